# Optimizing a Trainium2 kernel written in Bass

```python
import math
import jax
import jax.numpy as jnp
from jax import lax
import numpy as np

D_MODEL = 2048
BATCH = 4
SEQ = 2048
DEPTH = 2
DEC_BATCH = 128
DEC_SEQ = 1
PAST_LEN = 8192
PAGE_SIZE = 128

MEM_LEN = 256
CHUNK = 128
G_GROUPS = 8
G_WIDTH = 768
G_GDIM = G_WIDTH // G_GROUPS
SWA_HEADS = 12
SWA_KV = 4
SWA_HD = 64
SWA_WIDTH = SWA_HEADS * SWA_HD
SWA_KV_WIDTH = SWA_KV * SWA_HD
WINDOW = 128
QBLOCK = 128
MEM_HEADS = 4
MEM_HD = 128
MEM_WIDTH = MEM_HEADS * MEM_HD
N_BRANCH = 3
D_FF = 5632
CONV_W = 3
EPS = 1e-6
N_IN = 2 * G_WIDTH + SWA_WIDTH + 2 * SWA_KV_WIDTH + MEM_WIDTH + N_BRANCH * D_MODEL

kernel_name = 'hybrid_gmlp_swa_mem_convffn_step'


def rmsnorm(x, g):
    xf = x.astype(jnp.float32)
    y = xf * lax.rsqrt(jnp.mean(xf * xf, axis=-1, keepdims=True) + EPS)
    return (y * g.astype(jnp.float32)).astype(x.dtype)


def alibi_slopes(n):
    p = 2 ** int(math.floor(math.log2(n)))
    base = [2.0 ** (-8.0 * (i + 1) / p) for i in range(p)]
    extra = [2.0 ** (-8.0 * (2 * i + 1) / (2 * p)) for i in range(n - p)]
    return jnp.asarray(base + extra, dtype=jnp.float32)


def gmlp_spatial(v, ws, bs):
    B, T, _ = v.shape
    L = min(CHUNK, T)
    nc = -(-T // L)
    tp = nc * L
    if tp > T:
        v = jnp.pad(v, ((0, 0), (0, tp - T), (0, 0)))
    vr = v.reshape(B, nc, L, G_GROUPS, G_GDIM)
    w = jnp.tril(ws[:, :L, :L])
    m = jnp.einsum('gts,bnsgc->bntgc', w, vr) + jnp.transpose(bs[:, :L])[None, None, :, :, None]
    return m.reshape(B, tp, G_WIDTH)[:, :T]


def swa_attention(q, k, v, prefix_k, prefix_v, start, sinks, slopes):
    B, T = q.shape[0], q.shape[1]
    G = SWA_HEADS // SWA_KV
    P = prefix_k.shape[1]
    if P < WINDOW:
        padw = ((0, 0), (WINDOW - P, 0), (0, 0), (0, 0))
        prefix_k = jnp.pad(prefix_k, padw)
        prefix_v = jnp.pad(prefix_v, padw)
    kcat = jnp.concatenate([prefix_k.astype(k.dtype), k], axis=1)
    vcat = jnp.concatenate([prefix_v.astype(v.dtype), v], axis=1)
    new_k_buf = kcat[:, T:]
    new_v_buf = vcat[:, T:]
    qb = min(QBLOCK, T)
    nb = -(-T // qb)
    tp = nb * qb
    if tp > T:
        q = jnp.pad(q, ((0, 0), (0, tp - T), (0, 0), (0, 0)))
        kcat = jnp.pad(kcat, ((0, 0), (0, tp - T), (0, 0), (0, 0)))
        vcat = jnp.pad(vcat, ((0, 0), (0, tp - T), (0, 0), (0, 0)))
    kw = WINDOW + qb
    idx = jnp.arange(nb)[:, None] * qb + jnp.arange(kw)[None, :]
    kb = kcat[:, idx]
    vb = vcat[:, idx]
    qr = q.reshape(B, nb, qb, SWA_KV, G, SWA_HD)
    s = jnp.einsum('bnqkgd,bnskd->bnkgqs', qr, kb).astype(jnp.float32) * (SWA_HD ** -0.5)
    qpos = start + jnp.arange(tp).reshape(nb, qb)
    kpos = start - WINDOW + idx
    dist = qpos[:, :, None] - kpos[:, None, :]
    valid = (dist >= 0) & (dist <= WINDOW) & (kpos[:, None, :] >= 0)
    bias = -slopes.reshape(SWA_KV, G)[None, :, :, None, None] * dist[:, None, None].astype(jnp.float32)
    s = jnp.where(valid[:, None, None], s + bias, -jnp.inf)
    sink = jnp.broadcast_to(sinks.astype(jnp.float32).reshape(1, 1, SWA_KV, G, 1, 1), s.shape[:-1] + (1,))
    p = jax.nn.softmax(jnp.concatenate([s, sink], axis=-1), axis=-1)[..., :-1]
    o = jnp.einsum('bnkgqs,bnskd->bnqkgd', p.astype(vb.dtype), vb)
    return o.reshape(B, tp, SWA_WIDTH)[:, :T], new_k_buf, new_v_buf


def memory_attention(q, mk, mv):
    B, T = q.shape[0], q.shape[1]
    s = jnp.einsum('bthd,bmhd->bhtm', q, mk.astype(q.dtype)).astype(jnp.float32) * (MEM_HD ** -0.5)
    p = jax.nn.softmax(s, axis=-1).astype(q.dtype)
    return jnp.einsum('bhtm,bmhd->bthd', p, mv.astype(q.dtype)).reshape(B, T, MEM_WIDTH)


def layer_forward(x, start, swa_pk, swa_pv, mem_k, mem_v, conv_prefix, slopes,
                  norm_mix_g, w_in, gmlp_norm_g, gmlp_ws, gmlp_bs, attn_sinks,
                  w_br_g, w_br_a, w_br_m, w_out, norm_ffn_g, w_up, conv_w, conv_b, w_down):
    B, T, _ = x.shape
    xn = rmsnorm(x, norm_mix_g)
    z = xn @ w_in
    c1 = G_WIDTH
    c2 = c1 + G_WIDTH
    c3 = c2 + SWA_WIDTH
    c4 = c3 + SWA_KV_WIDTH
    c5 = c4 + SWA_KV_WIDTH
    c6 = c5 + MEM_WIDTH
    zu, zv, zq, zk, zvv, zm, zg = jnp.split(z, [c1, c2, c3, c4, c5, c6], axis=-1)
    u = jax.nn.gelu(zu)
    vg = rmsnorm(jax.nn.gelu(zv), gmlp_norm_g)
    o_g = u * gmlp_spatial(vg, gmlp_ws, gmlp_bs)
    c0 = ((start + T - 1) // CHUNK) * CHUNK - start
    v_rows = vg[:, c0:]
    q = zq.reshape(B, T, SWA_HEADS, SWA_HD)
    k = zk.reshape(B, T, SWA_KV, SWA_HD)
    v = zvv.reshape(B, T, SWA_KV, SWA_HD)
    o_a, kbuf, vbuf = swa_attention(q, k, v, swa_pk, swa_pv, start, attn_sinks, slopes)
    o_m = memory_attention(zm.reshape(B, T, MEM_HEADS, MEM_HD), mem_k, mem_v)
    gates = jax.nn.sigmoid(zg.astype(jnp.float32)).astype(x.dtype).reshape(B, T, N_BRANCH, D_MODEL)
    merged = (gates[:, :, 0] * (o_g @ w_br_g) + gates[:, :, 1] * (o_a @ w_br_a)
              + gates[:, :, 2] * (o_m @ w_br_m))
    x = x + merged @ w_out
    h = rmsnorm(x, norm_ffn_g) @ w_up
    hc = jnp.concatenate([conv_prefix.astype(h.dtype), h], axis=1)
    hconv = conv_b
    for j in range(CONV_W):
        hconv = hconv + hc[:, j:j + T] * conv_w[j]
    a, b = jnp.split(hconv, 2, axis=-1)
    x = x + (jax.nn.gelu(a) * b) @ w_down
    conv_buf = hc[:, -(CONV_W - 1):]
    return x, kbuf, vbuf, v_rows, conv_buf


def setup_inputs(seed: int = 0) -> dict:
    key = jax.random.key(seed)
    ks = jax.random.split(key, 32)

    def nrm(k, shape, scale):
        return jax.random.normal(k, shape, jnp.float32) * scale

    win_buf = min(WINDOW, PAST_LEN)
    return {
        'x_prompt': nrm(ks[0], (BATCH, SEQ, D_MODEL), 1.0),
        'x_sample': nrm(ks[1], (DEC_BATCH, DEC_SEQ, D_MODEL), 1.0),
        'cache_swa_k': nrm(ks[2], (DEPTH, DEC_BATCH, win_buf, SWA_KV, SWA_HD), 1.0),
        'cache_swa_v': nrm(ks[3], (DEPTH, DEC_BATCH, win_buf, SWA_KV, SWA_HD), 1.0),
        'cache_mem_k': nrm(ks[4], (DEPTH, DEC_BATCH, MEM_LEN, MEM_HEADS, MEM_HD), 1.0),
        'cache_mem_v': nrm(ks[5], (DEPTH, DEC_BATCH, MEM_LEN, MEM_HEADS, MEM_HD), 1.0),
        'state_conv': nrm(ks[6], (DEPTH, DEC_BATCH, CONV_W - 1, 2 * D_FF), 1.0),
        'mem_prompt': nrm(ks[7], (BATCH, MEM_LEN, D_MODEL), 1.0),
        'norm_mix_g': 1.0 + nrm(ks[8], (DEPTH, D_MODEL), 0.02),
        'w_in': nrm(ks[9], (DEPTH, D_MODEL, N_IN), D_MODEL ** -0.5),
        'gmlp_norm_g': 1.0 + nrm(ks[10], (DEPTH, G_WIDTH), 0.02),
        'gmlp_ws': nrm(ks[11], (DEPTH, G_GROUPS, CHUNK, CHUNK), 0.5 * CHUNK ** -0.5),
        'gmlp_bs': 1.0 + nrm(ks[12], (DEPTH, G_GROUPS, CHUNK), 0.02),
        'attn_sinks': nrm(ks[13], (DEPTH, SWA_HEADS), 0.5),
        'mem_norm_g': 1.0 + nrm(ks[14], (DEPTH, D_MODEL), 0.02),
        'w_mem_kv': nrm(ks[15], (DEPTH, D_MODEL, 2 * MEM_WIDTH), D_MODEL ** -0.5),
        'w_br_g': nrm(ks[16], (DEPTH, G_WIDTH, D_MODEL), G_WIDTH ** -0.5),
        'w_br_a': nrm(ks[17], (DEPTH, SWA_WIDTH, D_MODEL), SWA_WIDTH ** -0.5),
        'w_br_m': nrm(ks[18], (DEPTH, MEM_WIDTH, D_MODEL), MEM_WIDTH ** -0.5),
        'w_out': nrm(ks[19], (DEPTH, D_MODEL, D_MODEL), D_MODEL ** -0.5),
        'norm_ffn_g': 1.0 + nrm(ks[20], (DEPTH, D_MODEL), 0.02),
        'w_up': nrm(ks[21], (DEPTH, D_MODEL, 2 * D_FF), D_MODEL ** -0.5),
        'conv_w': nrm(ks[22], (DEPTH, CONV_W, 2 * D_FF), CONV_W ** -0.5),
        'conv_b': nrm(ks[23], (DEPTH, 2 * D_FF), 0.02),
        'w_down': nrm(ks[24], (DEPTH, D_FF, D_MODEL), D_FF ** -0.5),
        'final_norm_g': 1.0 + nrm(ks[25], (D_MODEL,), 0.02),
    }


def reference(x_prompt, x_sample, cache_swa_k, cache_swa_v, cache_mem_k, cache_mem_v, state_conv,
              mem_prompt, norm_mix_g, w_in, gmlp_norm_g, gmlp_ws, gmlp_bs, attn_sinks, mem_norm_g,
              w_mem_kv, w_br_g, w_br_a, w_br_m, w_out, norm_ffn_g, w_up, conv_w, conv_b, w_down,
              final_norm_g):
    slopes = alibi_slopes(SWA_HEADS)
    xp = x_prompt
    xs = x_sample
    Bp = xp.shape[0]
    pk, pv, sk, sv, mkp, mvp, gvp, gvs, cvp, cvs = ([] for _ in range(10))
    zero_kv = jnp.zeros((Bp, 0, SWA_KV, SWA_HD), xp.dtype)
    zero_conv = jnp.zeros((Bp, CONV_W - 1, 2 * D_FF), xp.dtype)
    for l in range(DEPTH):
        lw = dict(norm_mix_g=norm_mix_g[l], w_in=w_in[l], gmlp_norm_g=gmlp_norm_g[l],
                  gmlp_ws=gmlp_ws[l], gmlp_bs=gmlp_bs[l], attn_sinks=attn_sinks[l],
                  w_br_g=w_br_g[l], w_br_a=w_br_a[l], w_br_m=w_br_m[l], w_out=w_out[l],
                  norm_ffn_g=norm_ffn_g[l], w_up=w_up[l], conv_w=conv_w[l], conv_b=conv_b[l],
                  w_down=w_down[l])
        mkv = rmsnorm(mem_prompt, mem_norm_g[l]) @ w_mem_kv[l]
        mk = mkv[..., :MEM_WIDTH].reshape(Bp, MEM_LEN, MEM_HEADS, MEM_HD)
        mv = mkv[..., MEM_WIDTH:].reshape(Bp, MEM_LEN, MEM_HEADS, MEM_HD)
        xp, kb, vb, gv, cb = layer_forward(xp, 0, zero_kv, zero_kv, mk, mv, zero_conv, slopes, **lw)
        pk.append(kb); pv.append(vb); mkp.append(mk); mvp.append(mv); gvp.append(gv); cvp.append(cb)
        xs, kb, vb, gv, cb = layer_forward(xs, PAST_LEN, cache_swa_k[l], cache_swa_v[l], cache_mem_k[l],
                                           cache_mem_v[l], state_conv[l], slopes, **lw)
        sk.append(kb); sv.append(vb); gvs.append(gv); cvs.append(cb)
    y_prompt = rmsnorm(xp, final_norm_g)
    y_sample = rmsnorm(xs, final_norm_g)
    return (y_prompt, y_sample, jnp.stack(pk), jnp.stack(pv), jnp.stack(sk), jnp.stack(sv),
            jnp.stack(mkp), jnp.stack(mvp), jnp.stack(gvp), jnp.stack(gvs), jnp.stack(cvp), jnp.stack(cvs))
```

```python
import math
from contextlib import ExitStack
import numpy as np
import concourse.bass as bass
import concourse.mybir as mybir
from concourse.bass_utils import run_bass_kernel_spmd

F32 = mybir.dt.float32
BF16 = mybir.dt.bfloat16
AF = mybir.ActivationFunctionType
ALU = mybir.AluOpType
AX = mybir.AxisListType

DEPTH = 2
D = 2048
KC = 16
GW = 768
NG = 8
GD = 96
HQ = 12
HKV = 4
HD = 64
MH = 4
MHD = 128
MEM = 256
DFF = 5632
FC = 88
FA = 44
FH = 22
NIN = 9472
C_U, C_V, C_Q, C_K, C_VV, C_M, C_G = 0, 768, 1536, 2304, 2560, 2816, 3328
NCH = 9
TCH = 5
TP = TCH * 128
TC = (5, 4)
SCH = 5
TP1 = TC[1] * 128
NS = 16
NCORES = 8
EPS = 1e-6
GELU = AF.Gelu_apprx_tanh
ESZ = {F32: 4, BF16: 2}


def alibi_slopes(n):
    p = 2 ** int(math.floor(math.log2(n)))
    base = [2.0 ** (-8.0 * (i + 1) / p) for i in range(p)]
    extra = [2.0 ** (-8.0 * (2 * i + 1) / (2 * p)) for i in range(n - p)]
    return base + extra


class Op:
    __slots__ = ("id", "eng", "fn", "deps", "dma_key", "dma_cum", "signal", "sigcount")


class Prog:
    GRAN = 32
    ENGS = ("pe", "act", "dve", "pool", "sp")

    def __init__(self, nc):
        self.nc = nc
        self.ops = []
        self.cells = {}
        self.dma_tot = {}
        self.last_out = {}
        self.eng_ops = {e: [] for e in self.ENGS}

    @staticmethod
    def region(ap):
        name = ap.tensor.name
        esz = ESZ.get(ap.dtype, 4)
        dims = ap.ap
        pstride = dims[0][0]
        off = ap.offset % pstride if pstride > 0 else ap.offset
        ext = 1
        for st, n in dims[1:]:
            ext += (n - 1) * abs(st)
        return name, off * esz, (off + ext) * esz

    def _cells(self, ap):
        name, lo, hi = self.region(ap)
        if name.startswith("psb"):
            return [(name, 0)]
        g = self.GRAN
        return [(name, i) for i in range(lo // g, (hi - 1) // g + 1)]

    def op(self, eng, fn, reads=(), writes=(), dma_key=None):
        o = Op()
        o.id = len(self.ops)
        o.eng = eng
        o.fn = fn
        o.deps = set()
        o.dma_key = dma_key
        o.signal = False
        o.sigcount = 0
        o.dma_cum = 0
        if dma_key is not None:
            self.dma_tot[dma_key] = self.dma_tot.get(dma_key, 0) + 16
            o.dma_cum = self.dma_tot[dma_key]
            if dma_key.startswith("out") or dma_key.startswith("wk"):
                prev = self.last_out.get(dma_key)
                if prev is not None:
                    o.deps.add(prev)
                self.last_out[dma_key] = o.id
        reads = list(reads)
        writes = list(writes)
        writes = writes + [ap for ap in reads if ap.tensor.name.startswith("psb")]
        reads = [ap for ap in reads if not ap.tensor.name.startswith("psb")]
        for ap in reads:
            for c in self._cells(ap):
                st = self.cells.get(c)
                if st is None:
                    st = [None, []]
                    self.cells[c] = st
                if st[0] is not None:
                    o.deps.add(st[0])
                st[1].append(o.id)
        for ap in writes:
            for c in self._cells(ap):
                st = self.cells.get(c)
                if st is None:
                    st = [None, []]
                    self.cells[c] = st
                if st[0] is not None:
                    o.deps.add(st[0])
                for r in st[1]:
                    if r != o.id:
                        o.deps.add(r)
                st[0] = o.id
                st[1] = []
        o.deps.discard(o.id)
        self.ops.append(o)
        self.eng_ops[eng].append(o)
        return o

    def emit(self, stack):
        nc = self.nc
        ops = self.ops
        for o in ops:
            latest = {}
            for d in o.deps:
                a = ops[d]
                if a.dma_key is None and not (a.eng == "pe" and o.eng == "pe"):
                    if d > latest.get(a.eng, -1):
                        latest[a.eng] = d
            for d in latest.values():
                ops[d].signal = True
            o.deps = set(d for d in o.deps if ops[d].dma_key is not None or d in latest.values())
        cnt = {e: 0 for e in self.ENGS}
        for o in ops:
            if o.dma_key is None and o.signal:
                cnt[o.eng] += 1
                o.sigcount = cnt[o.eng]
        sem_eng = {e: stack.enter_context(nc.semaphore("s_" + e)) for e in self.ENGS}
        sem_dma = {k: stack.enter_context(nc.semaphore("d_" + k)) for k in self.dma_tot}
        self.final_waits = [(sem_dma[k], v) for k, v in self.dma_tot.items()]
        block = stack.enter_context(nc.Block())

        def run(engname, e):
            waited = {}
            for o in self.eng_ops[engname]:
                need = {}
                for d in o.deps:
                    a = ops[d]
                    if a.dma_key is not None:
                        key = ("d", a.dma_key)
                        val = a.dma_cum
                    else:
                        if a.eng == "pe" and engname == "pe":
                            continue
                        key = ("e", a.eng)
                        val = a.sigcount
                    if val > need.get(key, 0):
                        need[key] = val
                for key, val in need.items():
                    if waited.get(key, 0) >= val:
                        continue
                    waited[key] = val
                    sem = sem_dma[key[1]] if key[0] == "d" else sem_eng[key[1]]
                    e.wait_ge(sem, val)
                ins = o.fn(e)
                if o.dma_key is not None:
                    ins.then_inc(sem_dma[o.dma_key], 16)
                elif o.signal:
                    ins.then_inc(sem_eng[engname], 1)
            if engname == "sp":
                for sem, v in self.final_waits:
                    e.wait_ge(sem, v)

        @block.tensor
        def _(e):
            run("pe", e)

        @block.scalar
        def _(e):
            run("act", e)

        @block.vector
        def _(e):
            run("dve", e)

        @block.gpsimd
        def _(e):
            run("pool", e)

        @block.sync
        def _(e):
            run("sp", e)


def build(with_samples=True):
    nc = bass.Bass("TRN2", target_bir_lowering=False)
    nc.dge_precook = False
    pg = Prog(nc)
    slopes = alibi_slopes(HQ)

    def din(name, shape, dt=F32):
        return nc.dram_tensor(name, list(shape), dt, kind="ExternalInput").ap()

    def dout(name, shape):
        return nc.dram_tensor(name, list(shape), F32, kind="ExternalOutput").ap()

    xp_d = din("xp", [NCH * 128, D])
    xs_d = din("xs", [NS, D])
    memp_d = din("memp", [MEM, D])
    cswk_d = din("cswk", [DEPTH, NS, 128, 256])
    cswv_d = din("cswv", [DEPTH, NS, 128, 256])
    cmk_d = din("cmk", [DEPTH, NS, MEM, 512])
    cmv_d = din("cmv", [DEPTH, NS, MEM, 512])
    sconv_d = din("sconv", [DEPTH, NS, 2, 2 * DFF])
    gains_d = din("gains", [7 * KC, 128])
    w_in_d = din("w_in", [DEPTH, D, NIN])
    ggn_d = din("gmlp_norm_g", [DEPTH, GW])
    gws_d = din("gmlp_ws", [DEPTH, NG, 128, 128])
    gbs_d = din("gmlp_bs", [DEPTH, NG * 128])
    sinks_d = din("attn_sinks", [1, DEPTH * HQ])
    wmem_d = din("w_mem_kv", [DEPTH, D, 2 * 512])
    wbg_d = din("w_br_g", [DEPTH, GW, D])
    wba_d = din("w_br_a", [DEPTH, GW, D])
    wbm_d = din("w_br_m", [DEPTH, 512, D])
    wout_d = din("w_out", [DEPTH, D, D])
    wup_d = din("w_up", [DEPTH, D, 2 * DFF])
    cwb_d = din("cwb", [DEPTH, 4, FC, 128])
    wdn_d = din("w_down", [DEPTH, DFF, D])
    ident_d = din("ident", [128, 128])
    triu_d = din("triu", [128, 128])
    dist_d = din("distm", [128, 2, 128])
    dist0_d = din("dist0", [128, 128])
    xl_d = din("xlight", [128, D])
    y_d = dout("y", [NCH * 128 + NS, D])
    kp_d = dout("kp", [DEPTH, 128, 256])
    vp_d = dout("vp", [DEPTH, 128, 256])
    ks_d = dout("ks", [DEPTH, NS, 128, 256])
    vs_d = dout("vs", [DEPTH, NS, 128, 256])
    mko_d = dout("mko", [DEPTH, MEM, 512])
    mvo_d = dout("mvo", [DEPTH, MEM, 512])
    gvp_d = dout("gvp", [DEPTH, 128, GW])
    gvs_d = dout("gvs", [DEPTH, NS, GW])
    cvp_d = dout("cvp", [DEPTH, 2, 2 * DFF])
    cvs_d = dout("cvs", [DEPTH, NS, 2, 2 * DFF])
    qscr_d = nc.dram_tensor("qscr", [DEPTH, NS, GW + 512], F32).ap()

    stack = ExitStack()

    def sb(name, shape, dt):
        return stack.enter_context(nc.sbuf_tensor("sb_" + name, list(shape), dt))

    xres = sb("xres", [128, TCH + 1, D], F32)
    xnT = sb("xnT", [128, KC, TP + NS], BF16)
    NCOL = TP + NS
    R_U = 0
    R_OA = R_U + NG * NCOL
    R_OM = R_OA + 6 * NCOL
    R_MG = R_OM + 4 * NCOL
    R_VG = R_MG
    R_Q = R_VG + 6 * GW
    R_KA = R_Q + 6 * NCOL
    R_KB = R_KA + 2 * (128 + NCOL)
    R_V = R_KB + 2 * (128 + NCOL)
    R_QM = ((R_V + 7 * 4 * 66 + 15) // 16) * 16
    R_END1 = ((max(R_QM + 4 * NCOL, R_MG + KC * NCOL) + 15) // 16) * 16
    R_ACT = 0
    R_HST = R_ACT + FH * NCOL
    HSTN = ((2 * (2 + NCOL) * 2 + 15) // 16) * 16
    assert R_HST + 2 * HSTN <= R_END1
    R_SCR = R_END1
    R_END = R_SCR + 4 * NCOL * 2
    Rbuf = sb("Rbuf", [128, R_END], BF16)

    def rview(off, shape, parts=128):
        n = 1
        for s in shape:
            n *= s
        v = Rbuf[0:parts, off:off + n]
        if len(shape) == 2:
            return v.rearrange("p (a b) -> p a b", a=shape[0])
        if len(shape) == 3:
            return v.rearrange("p (a b c) -> p a b c", a=shape[0], b=shape[1])
        return v

    uT = rview(R_U, [NG, NCOL])
    oaT = rview(R_OA, [6, NCOL])
    omT = rview(R_OM, [4, NCOL])
    mgT = rview(R_MG, [KC, NCOL])
    vg = rview(R_VG, [6, GW])
    qT = rview(R_Q, [6, NCOL])
    kAT = rview(R_KA, [2, 128 + NCOL])
    kBT = rview(R_KB, [2, 128 + NCOL])
    v1 = rview(R_V, [7, 4, 66])
    qmT = rview(R_QM, [4, NCOL])
    actT = rview(R_ACT, [FH, NCOL])
    hst = [Rbuf[:, R_HST + i * HSTN: R_HST + i * HSTN + 2 * (2 + NCOL) * 2].bitcast(F32)
           .rearrange("p (a n) -> p a n", a=2) for i in range(2)]
    scr32 = [Rbuf[:, R_SCR + i * 2 * NCOL: R_SCR + (i + 1) * 2 * NCOL].bitcast(F32) for i in range(4)]
    xsbs = [Rbuf[:, R_SCR + 4 * NCOL: R_SCR + 4 * NCOL + D], Rbuf[:, R_SCR: R_SCR + D]]
    xsb_ctr = [0]
    junk = xsbs[0]

    NB = 5
    wslots = [sb("wslot%d" % i, [128, 4096], BF16) for i in range(NB)]
    small = sb("small", [128, 64], F32)
    gcols = sb("gcols", [128, 7 * KC], F32)
    cwT = sb("cwT", [128, DEPTH, 4, FC], F32)
    ggn = Rbuf[:, R_OA + 2048:R_OA + 2048 + 2 * GW].bitcast(F32)
    esink = sb("esink", [128, DEPTH * HQ], F32)
    ident = sb("identb", [128, 128], BF16)
    ident32 = sb("ident32", [128, 128], F32)
    triu = sb("triu", [128, 128], F32)
    distm = sb("distm", [128, 2, 128], F32)
    distm0 = sb("distm0", [128, 2, 128], F32)
    ones_b = sb("ones_b", [128, 128], BF16)
    epsT = sb("epsT", [128, 1], F32)
    wsT = sb("wsT", [128, NG, 128], BF16)
    bsrow = sb("bsrow", [1, NG * 128], BF16)
    mkT = sb("mkT", [128, DEPTH, MH, MEM], BF16)
    mv = sb("mv", [128, DEPTH, 2, 512], BF16)
    halo_kA = sb("halo_kA", [128, DEPTH, 2, 128], BF16)
    halo_kB = sb("halo_kB", [128, DEPTH, 2, 128], BF16)
    halo_v = sb("halo_v", [128, DEPTH, 4, 66], BF16)
    hprev = sb("hprev", [128, DEPTH, FC, 2], F32)
    Ebuf = [sb("Ebuf%d" % i, [128, 2, 128], BF16) for i in range(4)]
    Emem = [sb("Emem%d" % i, [128, 2, 328], BF16) for i in range(2)]
    oatok = sb("oatok", [128, GW], BF16)
    rden = sb("rden", [128, 16], F32)
    rmem = [sb("rmem%d" % i, [128, 328], F32) for i in range(2)]
    stage32 = Rbuf[:, R_OA:R_OA + 2048].bitcast(F32)
    wsraw = stage32.rearrange("p (g s) -> p g s", g=NG)
    memtok = xres[:, 0:2, :]
    cwrow = stage32[0:FC, 0:512].rearrange("p (j f) -> p j f", j=4)
    gainrow = stage32[0:7 * KC, 512:640]

    qTs = sb("qTs", [64, HQ, NS], BF16)
    kTs = sb("kTs", [64, HKV, NS], BF16)
    vTs = sb("vTs", [64, HKV, NS], BF16)
    shiftO = sb("shiftO", [64, 128], BF16)
    biasS = sb("biasS", [128, HQ], F32)
    psb = [stack.enter_context(nc.psum_tensor("psb%d" % i, [128, 512], F32)) for i in range(8)]
    pctr = [0]

    def bank():
        b = psb[pctr[0] % 6]
        pctr[0] += 1
        return b

    smallctr = [0]

    def scol():
        i = smallctr[0] % 64
        smallctr[0] += 1
        return small[:, i:i + 1]

    dctr = [0]

    def dma(q, out, in_, key=None, track_r=(), track_w=(), slow=False):
        if key is None:
            key = "x%d" % dctr[0]
            dctr[0] += 1
        kw = {"allow_slow_non_contiguous": True} if slow else {}
        pg.op(q, lambda e: e.dma_start(out=out, in_=in_, **kw), reads=track_r, writes=track_w, dma_key=key)

    def load(q, out_sb, in_dram, key, slow=False):
        dma(q, out_sb, in_dram, key=key, track_w=[out_sb], slow=slow)

    octr = [0]

    def okey():
        octr[0] += 1
        return "out%d" % (octr[0] % 8)

    def store(out_dram, in_sb, slow=False):
        dma("sp", out_dram, in_sb, key=okey(), track_r=[in_sb], slow=slow)

    def mm(out, lhsT, rhs, start, stop):
        pg.op("pe", lambda e: e.matmul(out, lhsT=lhsT, rhs=rhs, start=start, stop=stop),
              reads=[lhsT, rhs], writes=[out])

    def tr(out, in_, idn):
        pg.op("pe", lambda e: e.transpose(out, in_, idn), reads=[in_, idn], writes=[out])

    def act(out, in_, func, scale=1.0, bias=None, accum=None):
        rd = [in_]
        wr = [out]
        kw = {}
        if bias is not None:
            kw["bias"] = bias
            if not isinstance(bias, float):
                rd.append(bias)
        if not isinstance(scale, float):
            rd.append(scale)
        if accum is not None:
            kw["accum_out"] = accum
            wr.append(accum)
        pg.op("act", lambda e: e.activation(out=out, in_=in_, func=func, scale=scale, **kw), reads=rd, writes=wr)

    def tt(out, in0, in1, op, eng="dve"):
        pg.op(eng, lambda e: e.tensor_tensor(out=out, in0=in0, in1=in1, op=op), reads=[in0, in1], writes=[out])

    def ts(out, in0, s1, s2, op0, op1=None, eng="dve"):
        rd = [in0] + [s for s in (s1, s2) if s is not None and not isinstance(s, float)]
        if op1 is None:
            pg.op(eng, lambda e: e.tensor_scalar(out=out, in0=in0, scalar1=s1, scalar2=None, op0=op0),
                  reads=rd, writes=[out])
        else:
            pg.op(eng, lambda e: e.tensor_scalar(out=out, in0=in0, scalar1=s1, scalar2=s2, op0=op0, op1=op1),
                  reads=rd, writes=[out])

    def stt(out, in0, scalar, in1, op0, op1, eng="dve"):
        rd = [in0, in1] + ([] if isinstance(scalar, float) else [scalar])
        pg.op(eng, lambda e: e.scalar_tensor_tensor(out=out, in0=in0, scalar=scalar, in1=in1, op0=op0, op1=op1),
              reads=rd, writes=[out])

    def copy(out, in_, eng="dve"):
        if eng == "act":
            act(out, in_, AF.Copy)
        else:
            pg.op(eng, lambda e: e.tensor_copy(out=out, in_=in_), reads=[in_], writes=[out])

    def recip(out, in_):
        pg.op("dve", lambda e: e.reciprocal(out=out, in_=in_), reads=[in_], writes=[out])

    def memset(ap, val, eng="dve"):
        pg.op(eng, lambda e: e.memset(ap, val), writes=[ap])

    wctr = [0]

    def wload(src, kc, ncols, parts=128):
        i = wctr[0] % NB
        wctr[0] += 1
        dst = wslots[i][0:parts, 0:kc * ncols].rearrange("p (k n) -> p k n", k=kc)
        load("pool", dst, src, key="w%d" % i)
        return dst

    load("sp", ident32[:], ident_d, "c0")
    load("sp", triu[:], triu_d, "c1")
    load("sp", distm[:], dist_d, "c2")
    load("sp", distm0[:, 0, :], dist0_d, "c2b")
    load("sp", distm0[:, 1, :], dist_d[:, 1, :], "c2b")
    load("sp", gainrow[:], gains_d, "c3")
    load("sp", esink[:], sinks_d.to_broadcast([128, DEPTH * HQ]), "c5")
    copy(ident[:], ident32[:])
    memset(ones_b[:], 1.0)
    memset(epsT[:], EPS)
    memset(hprev[:], 0.0)
    act(esink[:], esink[:], AF.Exp)
    memset(shiftO[:], 0.0)
    copy(shiftO[:, 64:128], ident32[0:64, 0:64])
    for h in range(HQ):
        ts(biasS[:, h:h + 1], distm[:, 0, 0:1], -float(slopes[h]), None, ALU.mult)
    pb = bank()
    tr(pb[:, 0:7 * KC], gainrow[:], ident32[0:7 * KC, 0:7 * KC])
    copy(gcols[:], pb[:, 0:7 * KC])
    for l in range(DEPTH):
        load("sp", cwrow[:], cwb_d[l].rearrange("j k p -> k j p"), "c7")
        pb = bank()
        for j in range(4):
            tr(pb[:, j * FC:(j + 1) * FC], cwrow[:, j, :], ident32[0:FC, 0:FC])
        copy(cwT[:, l, :, :], pb[:, 0:4 * FC].rearrange("p (j k) -> p j k", j=4))

    G_MIX, G_FFN, G_MEM, G_FIN = 0, 2, 4, 6

    def gcol(gi, k0, n):
        return gcols[:, gi * KC + k0: gi * KC + k0 + n]

    def chunks_of(t):
        ch = [(c, 128, c * 128) for c in range(TC[t])]
        if t == 1 and with_samples:
            ch.append((SCH, NS, TP1))
        return ch

    def ncols_of(t):
        return TC[t] * 128 + (NS if (t == 1 and with_samples) else 0)

    def segs_of(t):
        n = ncols_of(t)
        h = n // 2
        return [(0, h), (h, n - h)]

    def norm_T(src_rows, n, gi, dstT, col0, extra=()):
        targets = [(gi, dstT, col0)] + list(extra)
        xsb = xsbs[xsb_ctr[0] % 2]
        xsb_ctr[0] += 1
        ss = scol()
        act(xsb[0:n, :], src_rows, AF.Square, accum=ss[0:n])
        rt = scol()
        act(rt[0:n], ss[0:n], AF.Sqrt, scale=1.0 / D, bias=epsT[0:n])
        rs = scol()
        recip(rs[0:n], rt[0:n])
        ts(xsb[0:n, :], src_rows, rs[0:n], None, ALU.mult)
        for half in range(2):
            pb = bank()
            pbv = pb[:].bitcast(BF16)[:, 0:8 * 128].rearrange("p (k t) -> p k t", k=8)
            for k in range(8):
                kk = half * 8 + k
                tr(pbv[:, k, 0:n], xsb[0:n, kk * 128:(kk + 1) * 128], ident[0:n, 0:n])
            for (g_i, d_T, c_0) in targets:
                tt(d_T[:, half * 8:half * 8 + 8, c_0:c_0 + n], pbv[:, :, 0:n],
                   gcol(g_i, half * 8, 8).unsqueeze(2).to_broadcast([128, 8, n]), ALU.mult)
        return rs

    def lin_ws(t, wblk, c0, m, rhsT, nk, evac, kparts=128):
        for (s0, sn) in segs_of(t):
            pb = bank()
            for k in range(nk):
                mm(pb[0:m, 0:sn], wblk[0:kparts, k, c0:c0 + m], rhsT[0:kparts, k, s0:s0 + sn], k == 0, k == nk - 1)
            evac(pb[0:m, 0:sn], s0, sn)

    w_in_v = [w_in_d[l].rearrange("(k p) n -> p k n", p=128) for l in range(DEPTH)]

    def mem_norm():
        load("sp", memtok[:], memp_d.rearrange("(c p) d -> p c d", p=128), "memtok")
        for c in range(2):
            norm_T(memtok[:, c, :], 128, G_MEM, xnT, c * 128, extra=[(G_MEM + 1, xnT, MEM + c * 128)])

    def mem_phase(l):
        mo = l * MEM
        wv = wmem_d[l].rearrange("(k p) n -> p k n", p=128)
        for blk in range(4):
            wb = wload(wv[:, :, blk * 256:(blk + 1) * 256], KC, 256)
            if blk < 2:
                for hh in range(2):
                    h = blk * 2 + hh
                    pb = bank()
                    for k in range(KC):
                        mm(pb[:, 0:MEM], wb[:, k, hh * 128:(hh + 1) * 128], xnT[:, k, mo:mo + MEM], k == 0, k == KC - 1)
                    copy(mkT[:, l, h, :], pb[:, 0:MEM], eng="act")
            for c in range(2):
                pb = bank()
                for k in range(KC):
                    mm(pb[:, 0:256], xnT[:, k, mo + c * 128:mo + (c + 1) * 128], wb[:, k, :], k == 0, k == KC - 1)
                si = (blk * 2 + c) % 4
                st = stage32[:, si * 256:(si + 1) * 256]
                copy(st, pb[:, 0:256], eng="act")
                if blk < 2:
                    store(mko_d[l, c * 128:(c + 1) * 128, blk * 256:(blk + 1) * 256], st)
                else:
                    b2 = blk - 2
                    copy(mv[:, l, c, b2 * 256:(b2 + 1) * 256], pb[:, 0:256])
                    store(mvo_d[l, c * 128:(c + 1) * 128, b2 * 256:(b2 + 1) * 256], st)

    def mixer(l, t):
        chs = chunks_of(t)
        ncol = ncols_of(t)
        last_tile = (t == 1)
        load("sp", wsraw, gws_d[l].rearrange("g t s -> t g s"), "wsraw")
        load("sp", ggn, ggn_d[l:l + 1, :].to_broadcast([128, GW]), "ggn")
        load("pool", bsrow[:], gbs_d[l:l + 1, :], "bsrow")
        for half in range(2):
            pb = bank()
            pbv = pb[:, :].rearrange("p (g t) -> p g t", g=4)
            for g in range(4):
                tr(pbv[:, g, :], wsraw[:, half * 4 + g, :], ident32[:])
            tt(wsT[:, half * 4:half * 4 + 4, :], pbv, triu[:].unsqueeze(1).to_broadcast([128, 4, 128]), ALU.mult)
        wv = w_in_v[l]
        if l == 0 and t == 0:
            load("sp", xres[:, SCH, :], xl_d, "xlight")
            norm_T(xres[:, SCH, :], 128, G_MIX + l, xnT, 0)
            wbl = wload(wv[:, :, C_K: C_K + 256], KC, 256)
            for j in range(2):
                pb = bank()
                for k in range(KC):
                    mm(pb[:, 0:128], wbl[:, k, j * 128:(j + 1) * 128], xnT[:, k, 0:128], k == 0, k == KC - 1)
                copy(kAT[:, j, 0:128], pb[:, 0:128], eng="act")
            i = wctr[0] % NB
            wctr[0] += 1
            wbx = wslots[i][:, 0:KC * 256].rearrange("p (k n) -> p k n", k=KC)
            for hp in range(2):
                for hh in range(2):
                    src = wv[:, :, C_K + (hp * 2 + (1 - hh)) * 64: C_K + (hp * 2 + (1 - hh)) * 64 + 64]
                    load("pool", wbx[:, :, (hp * 2 + hh) * 64:(hp * 2 + hh) * 64 + 64], src, key="wkL%d" % (hp * 2 + hh))
            for j in range(2):
                pb = bank()
                for k in range(KC):
                    mm(pb[:, 0:128], wbx[:, k, j * 128:(j + 1) * 128], xnT[:, k, 0:128], k == 0, k == KC - 1)
                copy(kBT[:, j, 0:128], pb[:, 0:128], eng="act")
            wbl = wload(wv[:, :, C_VV: C_VV + 256], KC, 256)
            pb = bank()
            for k in range(KC):
                mm(pb[:, 0:256], xnT[:, k, 0:128], wbl[:, k, :], k == 0, k == KC - 1)
            copy(v1[:, 0, :, 0:64], pb[:, 0:256].rearrange("p (h d) -> p h d", h=HKV), eng="act")
            memset(v1[:, 0, :, 64:65], 1.0)
        for (c, n, col0) in chs:
            norm_T(xres[0:n, c, :], n, G_MIX + l, xnT, col0)
        for b in range(3):
            wb = wload(wv[:, :, C_V + b * 256: C_V + (b + 1) * 256], KC, 256)
            for (c, n, col0) in chs:
                pb = bank()
                for k in range(KC):
                    mm(pb[0:n, 0:256], xnT[:, k, col0:col0 + n], wb[:, k, :], k == 0, k == KC - 1)
                act(vg[0:n, c, b * 256:(b + 1) * 256], pb[0:n, 0:256], GELU)
        for (c, n, col0) in chs:
            ss = scol()
            act(junk[0:n, 0:GW], vg[0:n, c, :], AF.Square, accum=ss[0:n])
            rt = scol()
            act(rt[0:n], ss[0:n], AF.Sqrt, scale=1.0 / GW, bias=epsT[0:n])
            rs = scol()
            recip(rs[0:n], rt[0:n])
            is_out = last_tile and (c == TC[1] - 1 or c == SCH)
            if is_out:
                st = stage32[0:n, 0:GW]
                stt(st, vg[0:n, c, :], rs[0:n], ggn[0:n, :], ALU.mult, ALU.mult)
                if c == SCH:
                    store(gvs_d[l], st)
                else:
                    store(gvp_d[l], st)
            stt(vg[0:n, c, :], vg[0:n, c, :], rs[0:n], ggn[0:n, :], ALU.mult, ALU.mult)
        for b in range(4):
            wb = wload(wv[:, :, C_U + b * 192: C_U + (b + 1) * 192], KC, 192)
            for gg in range(2):
                g = b * 2 + gg
                lin_ws(t, wb, gg * GD, GD, xnT, KC,
                       lambda ps, s0, sn, g=g: act(uT[0:GD, g, s0:s0 + sn], ps, GELU))
        for b in range(3):
            wb = wload(wv[:, :, C_Q + b * 256: C_Q + (b + 1) * 256], KC, 256)
            for j in range(2):
                qc = b * 2 + j
                lin_ws(t, wb, j * 128, 128, xnT, KC,
                       lambda ps, s0, sn, qc=qc: act(qT[:, qc, s0:s0 + sn], ps, AF.Copy, scale=0.125))
            if last_tile and with_samples:
                for hh in range(4):
                    pb = bank()
                    for k in range(KC):
                        mm(pb[0:64, 0:NS], wb[:, k, hh * 64:(hh + 1) * 64], xnT[:, k, TP1:TP1 + NS], k == 0, k == KC - 1)
                    act(qTs[:, b * 4 + hh, :], pb[0:64, 0:NS], AF.Copy, scale=0.125)
        wb = wload(wv[:, :, C_K: C_K + 256], KC, 256)
        for j in range(2):
            lin_ws(t, wb, j * 128, 128, xnT, KC,
                   lambda ps, s0, sn, j=j: copy(kAT[:, j, 128 + s0:128 + s0 + sn], ps, eng="act"))
        if last_tile and with_samples:
            for hh in range(4):
                pb = bank()
                for k in range(KC):
                    mm(pb[0:64, 0:NS], wb[:, k, hh * 64:(hh + 1) * 64], xnT[:, k, TP1:TP1 + NS], k == 0, k == KC - 1)
                copy(kTs[:, hh, :], pb[0:64, 0:NS], eng="act")
        if last_tile:
            for (c, n, col0) in chs[TC[1] - 1:]:
                pb = bank()
                for k in range(KC):
                    mm(pb[0:n, 0:256], xnT[:, k, col0:col0 + n], wb[:, k, :], k == 0, k == KC - 1)
                st = stage32[0:n, 256:512]
                copy(st, pb[0:n, 0:256], eng="act")
                if c == SCH:
                    store(ks_d[l, :, 127, :], st)
                else:
                    store(kp_d[l], st)
        i = wctr[0] % NB
        wctr[0] += 1
        wbs = wslots[i][:, 0:KC * 256].rearrange("p (k n) -> p k n", k=KC)
        for hp in range(2):
            for hh in range(2):
                src = wv[:, :, C_K + (hp * 2 + (1 - hh)) * 64: C_K + (hp * 2 + (1 - hh)) * 64 + 64]
                load("pool", wbs[:, :, (hp * 2 + hh) * 64:(hp * 2 + hh) * 64 + 64], src, key="wk%d" % (hp * 2 + hh))
        for j in range(2):
            lin_ws(t, wbs, j * 128, 128, xnT, KC,
                   lambda ps, s0, sn, j=j: copy(kBT[:, j, 128 + s0:128 + s0 + sn], ps, eng="act"))
        wb = wload(wv[:, :, C_VV: C_VV + 256], KC, 256)
        if last_tile and with_samples:
            for hh in range(4):
                pb = bank()
                for k in range(KC):
                    mm(pb[0:64, 0:NS], wb[:, k, hh * 64:(hh + 1) * 64], xnT[:, k, TP1:TP1 + NS], k == 0, k == KC - 1)
                copy(vTs[:, hh, :], pb[0:64, 0:NS], eng="act")
        for (c, n, col0) in chs:
            pb = bank()
            for k in range(KC):
                mm(pb[0:n, 0:256], xnT[:, k, col0:col0 + n], wb[:, k, :], k == 0, k == KC - 1)
            if c != SCH:
                copy(v1[0:n, 1 + c, :, 0:64], pb[0:n, 0:256].rearrange("p (h d) -> p h d", h=HKV), eng="act")
                memset(v1[0:n, 1 + c, :, 64:65], 1.0)
            if last_tile and (c == TC[1] - 1 or c == SCH):
                st = stage32[0:n, 512:768]
                copy(st, pb[0:n, 0:256], eng="act")
                if c == SCH:
                    store(vs_d[l, :, 127, :], st)
                else:
                    store(vp_d[l], st)
        for b in range(2):
            wb = wload(wv[:, :, C_M + b * 256: C_M + (b + 1) * 256], KC, 256)
            for j in range(2):
                h = b * 2 + j
                lin_ws(t, wb, j * 128, 128, xnT, KC,
                       lambda ps, s0, sn, h=h: act(qmT[:, h, s0:s0 + sn], ps, AF.Copy, scale=MHD ** -0.5))
        PHASES.append(("  attn t%d l%d" % (t, l), len(pg.eng_ops["pe"])))
        if t == 1:
            copy(kAT[:, :, 0:128], halo_kA[:, l, :, :])
            copy(kBT[:, :, 0:128], halo_kB[:, l, :, :])
            copy(v1[:, 0, :, 0:65], halo_v[:, l, :, 0:65])
        for c in range(TC[t]):
            col0 = c * 128
            for half in range(2):
                pb = bank()
                for gg in range(4):
                    g = half * 4 + gg
                    o = pb[0:GD, gg * 128:(gg + 1) * 128]
                    mm(o, vg[:, c, g * GD:(g + 1) * GD], wsT[:, g, :], True, False)
                    mm(o, ones_b[0:1, 0:GD], bsrow[0:1, g * 128:(g + 1) * 128], False, True)
                tt(uT[0:GD, half * 4:half * 4 + 4, col0:col0 + 128],
                   pb[0:GD, :].rearrange("p (g t) -> p g t", g=4),
                   uT[0:GD, half * 4:half * 4 + 4, col0:col0 + 128], ALU.mult)
            first = (t == 0 and c == 0)
            has_prev = (not first) or l == 0
            dm = distm0 if (first and l == 0) else distm
            pos = [psb[6], psb[7]]
            blocks = ([0] if has_prev else []) + [1]
            b0 = blocks[0]
            nb_ = len(blocks)

            LAG = 3

            def qk(h):
                kv = h // 3
                base = (h % 2) * 64
                kT = kAT if (kv % 2) == (h % 2) else kBT
                E = Ebuf[h % 4]
                pb = bank()
                for kb in blocks:
                    kcol = col0 + kb * 128
                    mm(pb[:, kb * 128:(kb + 1) * 128], kT[base:base + 64, kv // 2, kcol:kcol + 128],
                       qT[base:base + 64, h // 2, col0:col0 + 128], True, True)
                pbv = pb[:, b0 * 128:(b0 + nb_) * 128].rearrange("p (b t) -> p b t", b=nb_)
                stt(pbv, dm[:, b0:b0 + nb_, :], -float(slopes[h]), pbv, ALU.mult, ALU.add)
                act(E[:, b0:b0 + nb_, :], pbv, AF.Exp)

            def pvh(h):
                kv = h // 3
                E = Ebuf[h % 4]
                po = pos[h // 6][:, (h % 6) * 65:(h % 6) * 65 + 65]
                for ii, kb in enumerate(blocks):
                    mm(po, E[:, kb, :], v1[:, c + kb, kv, 0:65], ii == 0, ii == nb_ - 1)

            for h in range(HQ + LAG):
                if h < HQ:
                    qk(h)
                if h >= LAG:
                    pvh(h - LAG)
            for half in range(2):
                pv = pos[half][:, 0:6 * 65].rearrange("p (h e) -> p h e", h=6)
                rd = rden[:, half * 6:half * 6 + 6]
                tt(rd, pv[:, :, 64], esink[:, l * HQ + half * 6: l * HQ + half * 6 + 6], ALU.add)
                recip(rd, rd)
                tt(oatok[:, half * 384:(half + 1) * 384].rearrange("p (h d) -> p h d", h=6), pv[:, :, 0:64],
                   rd.unsqueeze(2).to_broadcast([128, 6, 64]), ALU.mult)
            pb = bank()
            pbv = pb[:].bitcast(BF16)[:, 0:6 * 128].rearrange("p (k t) -> p k t", k=6)
            for k in range(6):
                tr(pbv[:, k, :], oatok[:, k * 128:(k + 1) * 128], ident[:])
            copy(oaT[:, :, col0:col0 + 128], pbv, eng="act")
        if t == 0:
            copy(halo_kA[:, l, :, :], kAT[:, :, 128 + TP - 128:128 + TP])
            copy(halo_kB[:, l, :, :], kBT[:, :, 128 + TP - 128:128 + TP])
            copy(halo_v[:, l, :, 0:65], v1[:, TCH, :, 0:65])
        PHASES.append(("  mema t%d l%d" % (t, l), len(pg.eng_ops["pe"])))
        for (s0, sn) in [(0, TC[t] * 64), (TC[t] * 64, TC[t] * 64)]:
            def ms(h):
                E = Emem[h % 2]
                for mt in range(2):
                    pb = bank()
                    mm(pb[:, 0:sn], mkT[:, l, h, mt * 128:(mt + 1) * 128], qmT[:, h, s0:s0 + sn], True, True)
                    act(E[:, mt, 0:sn], pb[:, 0:sn], AF.Exp)

            def mp(h):
                E = Emem[h % 2]
                pd = bank()
                po = bank()
                for mt in range(2):
                    mm(pd[:, 0:sn], ones_b[:, :], E[:, mt, 0:sn], mt == 0, mt == 1)
                for mt in range(2):
                    mm(po[:, 0:sn], mv[:, l, mt, h * 128:(h + 1) * 128], E[:, mt, 0:sn], mt == 0, mt == 1)
                rm = rmem[h % 2]
                recip(rm[:, 0:sn], pd[:, 0:sn])
                tt(omT[:, h, s0:s0 + sn], po[:, 0:sn], rm[:, 0:sn], ALU.mult)

            for h in range(MH + 1):
                if h < MH:
                    ms(h)
                if h >= 1:
                    mp(h - 1)
        if last_tile and with_samples:
            PHASES.append(("  sattn t%d l%d" % (t, l), len(pg.eng_ops["pe"])))
            sample_attention(l)
            PHASES.append(("  merge t%d l%d" % (t, l), len(pg.eng_ops["pe"])))
        if not (last_tile and with_samples):
            PHASES.append(("  merge t%d l%d" % (t, l), len(pg.eng_ops["pe"])))
        wbg_v = wbg_d[l].rearrange("(g p) n -> p g n", p=GD)
        wba_v = wba_d[l].rearrange("(k p) n -> p k n", p=128)
        wbm_v = wbm_d[l].rearrange("(k p) n -> p k n", p=128)
        srcs = [(uT, NG, GD, wbg_v), (oaT, 6, 128, wba_v), (omT, 4, 128, wbm_v)]
        for jb in range(8):
            cs = slice(jb * 256, (jb + 1) * 256)
            for b in range(3):
                src, nk, kp, wsrc = srcs[b]
                wbr = wload(wsrc[:, :, cs], nk, 256, parts=kp)
                wg = wload(wv[:, :, C_G + b * D + jb * 256: C_G + b * D + (jb + 1) * 256], KC, 256)
                for jj in range(2):
                    j = jb * 2 + jj
                    acc = scr32[1 + jj]
                    for (s0, sn) in segs_of(t):
                        pgb = bank()
                        for k in range(KC):
                            mm(pgb[:, 0:sn], wg[:, k, jj * 128:(jj + 1) * 128], xnT[:, k, s0:s0 + sn],
                               k == 0, k == KC - 1)
                        sg = scr32[0 if (jj == 0) else 3]
                        act(sg[:, s0:s0 + sn], pgb[:, 0:sn], AF.Sigmoid)
                        ppb = bank()
                        for k in range(nk):
                            mm(ppb[:, 0:sn], wbr[0:kp, k, jj * 128:(jj + 1) * 128], src[0:kp, k, s0:s0 + sn],
                               k == 0, k == nk - 1)
                        if b == 0:
                            tt(acc[:, s0:s0 + sn], ppb[:, 0:sn], sg[:, s0:s0 + sn], ALU.mult)
                        else:
                            tt(sg[:, s0:s0 + sn], ppb[:, 0:sn], sg[:, s0:s0 + sn], ALU.mult)
                            if b == 1:
                                tt(acc[:, s0:s0 + sn], acc[:, s0:s0 + sn], sg[:, s0:s0 + sn], ALU.add)
                            else:
                                tt(mgT[:, j, s0:s0 + sn], acc[:, s0:s0 + sn], sg[:, s0:s0 + sn], ALU.add)
        PHASES.append(("  wout t%d l%d" % (t, l), len(pg.eng_ops["pe"])))
        wo_v = wout_d[l].rearrange("(k p) n -> p k n", p=128)
        for cb in range(8):
            wb = wload(wo_v[:, :, cb * 256:(cb + 1) * 256], KC, 256)
            for (c, n, col0) in chs:
                pb = bank()
                for k in range(KC):
                    mm(pb[0:n, 0:256], mgT[:, k, col0:col0 + n], wb[:, k, :], k == 0, k == KC - 1)
                tt(xres[0:n, c, cb * 256:(cb + 1) * 256], xres[0:n, c, cb * 256:(cb + 1) * 256], pb[0:n, 0:256],
                   ALU.add)

    def sample_attention(l):
        AL = R_VG
        cur = [AL + 12288]

        def sm(parts, shape, dt):
            n = 1
            for x_ in shape:
                n *= x_
            nel = n * (2 if dt == F32 else 1)
            v = Rbuf[0:parts, cur[0]:cur[0] + nel]
            cur[0] += ((nel + 15) // 16) * 16
            if dt == F32:
                v = v.bitcast(F32)
            if len(shape) == 2:
                v = v.rearrange("p (a b) -> p a b", a=shape[0])
            elif len(shape) == 3:
                v = v.rearrange("p (a b c) -> p a b c", a=shape[0], b=shape[1])
            return v

        ws00 = sm(16, [NG], F32)
        sI = sm(16, [NG, NS], BF16)
        bs0 = sm(1, [NG, NS], BF16)
        load("sp", ws00, gws_d[l][:, 0, 0:1].rearrange("g o -> o g").to_broadcast([NS, NG]), "ws00", slow=True)
        tt(sI, ident32[0:NS, 0:NS].unsqueeze(1).to_broadcast([NS, NG, NS]),
           ws00.unsqueeze(2).to_broadcast([NS, NG, NS]), ALU.mult)
        copy(bs0, bsrow[0:1, :].rearrange("o (g t) -> o g t", g=NG)[:, :, 0:1].to_broadcast([1, NG, NS]))
        pb = bank()
        for g in range(NG):
            o = pb[0:GD, g * NS:(g + 1) * NS]
            mm(o, vg[0:NS, SCH, g * GD:(g + 1) * GD], sI[:, g, :], True, False)
            mm(o, ones_b[0:1, 0:GD], bs0[0:1, g, :], False, True)
        tt(uT[0:GD, :, TP1:TP1 + NS], pb[0:GD, 0:NG * NS].rearrange("p (g b) -> p g b", g=NG),
           uT[0:GD, :, TP1:TP1 + NS], ALU.mult)
        cur_s1 = cur[0]
        cur[0] = AL + 7168
        Es = sm(128, [NS, HQ], BF16)
        sbS = sm(128, [4, HQ], F32)

        def s2slot(sl):
            o = AL + sl * 3584
            return (Rbuf[:, o:o + 1024].bitcast(F32).rearrange("p (b f) -> p b f", b=2),
                    Rbuf[:, o + 1024:o + 2048].bitcast(F32).rearrange("p (b f) -> p b f", b=2),
                    Rbuf[:, o + 2048:o + 2560].rearrange("p (b f) -> p b f", b=2),
                    Rbuf[0:64, o + 2560:o + 3584].rearrange("p (b k j) -> p b k j", b=2, k=4))

        psS = psb[7]
        psO = psb[6]
        NG2 = NS // 2

        def s2_load(i):
            K32, V32, Vb, KTs = s2slot(i % 2)
            load("sp", K32, cswk_d[l, i * 2:(i + 1) * 2].rearrange("b j f -> j b f"), "k32_%d" % (i % 2))
            load("sp", V32, cswv_d[l, i * 2:(i + 1) * 2].rearrange("b j f -> j b f"), "v32_%d" % (i % 2))

        def s2_tr(i):
            K32, V32, Vb, KTs = s2slot(i % 2)
            copy(Vb, V32)
            for bb in range(2):
                pk = bank()
                for kv in range(HKV):
                    tr(pk[0:64, kv * 128:(kv + 1) * 128], K32[:, bb, kv * 64:(kv + 1) * 64], ident32[:, :])
                copy(KTs[:, bb, :, :], pk[0:64, :].rearrange("p (k j) -> p k j", k=4), eng="act")

        def s2_sc(i):
            K32, V32, Vb, KTs = s2slot(i % 2)
            pS = bank()
            for bb in range(2):
                b = i * 2 + bb
                for kv in range(HKV):
                    mm(pS[:, bb * HQ + kv * 3: bb * HQ + kv * 3 + 3], KTs[:, bb, kv, :], qTs[:, kv * 3:(kv + 1) * 3, b],
                       True, True)
            sb_ = sbS[:, (i % 2) * 2:(i % 2) * 2 + 2, :]
            tt(sb_, pS[:, 0:2 * HQ].rearrange("p (b h) -> p b h", b=2),
               biasS[:, :].unsqueeze(1).to_broadcast([128, 2, HQ]), ALU.add)
            act(Es[:, i * 2:(i + 1) * 2, :], sb_, AF.Exp)

        def s2_pv(i):
            K32, V32, Vb, KTs = s2slot(i % 2)
            for bb in range(2):
                b = i * 2 + bb
                for kv in range(HKV):
                    mm(psO[0:64, b * HQ + kv * 3: b * HQ + kv * 3 + 3], Vb[:, bb, kv * 64:(kv + 1) * 64],
                       Es[:, b, kv * 3:(kv + 1) * 3], True, True)

        s2_load(0)
        for i in range(NG2 + 2):
            if i + 1 < NG2:
                s2_load(i + 1)
            if 0 <= i - 2 < NG2:
                s2_pv(i - 2)
            if 0 <= i - 1 < NG2:
                s2_sc(i - 1)
            if i < NG2:
                s2_tr(i)
        pd = bank()
        mm(pd[:, 0:NS * HQ], ones_b[:, :], Es.rearrange("p b h -> p (b h)"), True, True)
        prodb = sm(64, [HQ, NS], BF16)
        tt(prodb.rearrange("p (k g) b -> p k g b", k=HKV), qTs[:, :, :].rearrange("p (k g) b -> p k g b", k=HKV),
           kTs[:, :, :].unsqueeze(2).to_broadcast([64, HKV, 3, NS]), ALU.mult)
        pn = bank()
        mm(pn[:, 0:HQ * NS], ones_b[0:64, :], prodb.rearrange("p h b -> p (h b)"), True, True)
        enew = sm(64, [HQ, NS], F32)
        act(enew, pn[0:64, 0:HQ * NS].rearrange("p (h b) -> p h b", h=HQ), AF.Exp)
        dtot = sm(64, [NS, HQ], F32)
        tt(dtot, pd[0:64, 0:NS * HQ].rearrange("p (b h) -> p b h", b=NS), enew.rearrange("p h b -> p b h"), ALU.add)
        tt(dtot, dtot, esink[0:64, l * HQ:(l + 1) * HQ].unsqueeze(1).to_broadcast([64, NS, HQ]), ALU.add)
        recip(dtot, dtot)
        tmpo = sm(64, [NS, HQ], F32)
        tt(tmpo.rearrange("p b (k g) -> p b k g", k=HKV),
           enew.rearrange("p (k g) b -> p b k g", k=HKV),
           vTs[:, :, :].rearrange("p k b -> p b k").unsqueeze(3).to_broadcast([64, NS, HKV, 3]), ALU.mult)
        tt(tmpo, tmpo, psO[0:64, 0:NS * HQ].rearrange("p (b h) -> p b h", b=NS), ALU.add)
        oTs = sm(64, [NS, HQ], BF16)
        tt(oTs, tmpo, dtot, ALU.mult)
        pb = bank()
        for k in range(6):
            o = pb[:, k * NS:(k + 1) * NS]
            mm(o, ident[0:64, :], oTs[:, :, 2 * k], True, False)
            mm(o, shiftO[:, :], oTs[:, :, 2 * k + 1], False, True)
        copy(oaT[:, :, TP1:TP1 + NS], pb[:, 0:6 * NS].rearrange("p (k b) -> p k b", k=6), eng="act")
        def s3slot(sl):
            o = AL + sl * 6144
            return (Rbuf[:, o:o + 2048].bitcast(F32).rearrange("p (m f) -> p m f", m=2),
                    Rbuf[:, o + 2048:o + 4096].bitcast(F32).rearrange("p (m f) -> p m f", m=2),
                    Rbuf[:, o + 4096:o + 5120].rearrange("p (m f) -> p m f", m=2),
                    Rbuf[:, o + 5120:o + 6144].rearrange("p (h m j) -> p h m j", h=MH, m=2))

        assert cur[0] <= AL + 12288
        cur[0] = cur_s1
        Em = sm(128, [2, NS, MH], BF16)
        psSm = psb[7]
        psOm = psb[6]

        def s3_load(b):
            Km32, Vm32, Vmb, KTm = s3slot(b % 2)
            load("sp", Km32, cmk_d[l, b].rearrange("(m p) f -> p m f", p=128), "km32_%d" % (b % 2))
            load("sp", Vm32, cmv_d[l, b].rearrange("(m p) f -> p m f", p=128), "vm32_%d" % (b % 2))

        def s3_tr(b):
            Km32, Vm32, Vmb, KTm = s3slot(b % 2)
            copy(Vmb, Vm32)
            for hp in range(2):
                pk = bank()
                for hh in range(2):
                    h = hp * 2 + hh
                    for mt in range(2):
                        tr(pk[:, (hh * 2 + mt) * 128:(hh * 2 + mt + 1) * 128], Km32[:, mt, h * 128:(h + 1) * 128],
                           ident32[:, :])
                copy(KTm[:, hp * 2:hp * 2 + 2, :, :], pk[:, :].rearrange("p (h m j) -> p h m j", h=2, m=2), eng="act")

        def s3_sc(b):
            Km32, Vm32, Vmb, KTm = s3slot(b % 2)
            pS = bank()
            for h in range(MH):
                for mt in range(2):
                    cidx = mt * MH + h
                    mm(pS[:, cidx:cidx + 1], KTm[:, h, mt, :], qmT[:, h, TP1 + b:TP1 + b + 1], True, True)
            act(Em[:, :, b, :], pS[:, 0:2 * MH].rearrange("p (m h) -> p m h", m=2), AF.Exp)

        def s3_pv(b):
            Km32, Vm32, Vmb, KTm = s3slot(b % 2)
            for h in range(MH):
                for mt in range(2):
                    mm(psOm[:, 256 + b * MH + h: 256 + b * MH + h + 1], Vmb[:, mt, h * 128:(h + 1) * 128],
                       Em[:, mt, b, h:h + 1], mt == 0, mt == 1)

        s3_load(0)
        for i in range(NS + 2):
            if i + 1 < NS:
                s3_load(i + 1)
            if 0 <= i - 2 < NS:
                s3_pv(i - 2)
            if 0 <= i - 1 < NS:
                s3_sc(i - 1)
            if i < NS:
                s3_tr(i)
        pdm = bank()
        for mt in range(2):
            mm(pdm[:, 0:NS * MH], ones_b[:, :], Em[:, mt, :, :].rearrange("p b h -> p (b h)"), mt == 0, mt == 1)
        rdm = sm(128, [NS, MH], F32)
        assert cur[0] <= R_QM
        recip(rdm, pdm[:, 0:NS * MH].rearrange("p (b h) -> p b h", b=NS))
        tt(omT[:, :, TP1:TP1 + NS].rearrange("p h b -> p b h"),
           psOm[:, 256:256 + NS * MH].rearrange("p (b h) -> p b h", b=NS), rdm, ALU.mult)
        dma("sp", ks_d[l, :, 0:127, :], cswk_d[l, :, 1:128, :], key=okey())
        dma("sp", vs_d[l, :, 0:127, :], cswv_d[l, :, 1:128, :], key=okey())
        dma("sp", cvs_d[l, :, 0, :], sconv_d[l, :, 1, :], key=okey())

    def ffn(l, t):
        chs = chunks_of(t)
        tpt = TC[t] * 128
        ncol = ncols_of(t)
        last_tile = (t == 1)
        for (c, n, col0) in chs:
            norm_T(xres[0:n, c, :], n, G_FFN + l, xnT, col0)
        wu_v = wup_d[l].rearrange("(k p) n -> p k n", p=128)
        wd_v = wdn_d[l].rearrange("(k p) n -> p k n", p=128)
        for hf in range(2):
            pending = None
            for f in range(hf * FH, (hf + 1) * FH):
                if last_tile and with_samples:
                    fs_load(l, f)
                if f % 2 == 0:
                    wa = wload(wu_v[:, :, f * 128:(f + 2) * 128], KC, 256)
                    wbb = wload(wu_v[:, :, DFF + f * 128: DFF + (f + 2) * 128], KC, 256)
                hs = hst[f % 2]
                for ab, wb in enumerate((wa, wbb)):
                    fc = f + ab * FA
                    copy(hs[:, ab, 0:2], hprev[:, l, fc, :], eng="act")
                    lin_ws(t, wb, (f % 2) * 128, 128, xnT, KC,
                           lambda ps, s0, sn, ab=ab, hs=hs: copy(hs[:, ab, 2 + s0:2 + s0 + sn], ps, eng="act"))
                    copy(hprev[:, l, fc, :], hs[:, ab, tpt:tpt + 2], eng="act")
                    w0 = cwT[:, l, 0, fc:fc + 1]
                    w1 = cwT[:, l, 1, fc:fc + 1]
                    w2 = cwT[:, l, 2, fc:fc + 1]
                    bb = cwT[:, l, 3, fc:fc + 1]
                    cv = scr32[ab]
                    ts(cv[:, 0:tpt], hs[:, ab, 2:2 + tpt], w2, bb, ALU.mult, ALU.add)
                    stt(cv[:, 0:tpt], hs[:, ab, 1:1 + tpt], w1, cv[:, 0:tpt], ALU.mult, ALU.add)
                    stt(cv[:, 0:tpt], hs[:, ab, 0:tpt], w0, cv[:, 0:tpt], ALU.mult, ALU.add)
                act(scr32[0][:, 0:tpt], scr32[0][:, 0:tpt], GELU)
                tt(actT[:, f - hf * FH, 0:tpt], scr32[0][:, 0:tpt], scr32[1][:, 0:tpt], ALU.mult)
                if last_tile and with_samples:
                    if pending is not None:
                        ffn_samples(l, pending[0], pending[1])
                    pending = (f, hs)
            if pending is not None:
                ffn_samples(l, pending[0], pending[1])
            for cb in range(8):
                accs = {}
                for kg in range(2):
                    wb = wload(wd_v[:, hf * FH + kg * 11: hf * FH + (kg + 1) * 11, cb * 256:(cb + 1) * 256], 11, 256)
                    for (c, n, col0) in chs:
                        if kg == 0:
                            accs[c] = bank()
                        pb = accs[c]
                        for k in range(11):
                            kk = kg * 11 + k
                            mm(pb[0:n, 0:256], actT[:, kk, col0:col0 + n], wb[:, k, :], kk == 0, kk == FH - 1)
                for (c, n, col0) in chs:
                    tt(xres[0:n, c, cb * 256:(cb + 1) * 256], xres[0:n, c, cb * 256:(cb + 1) * 256],
                       accs[c][0:n, 0:256], ALU.add)

    RF = R_HST + 2 * HSTN
    scslab = [Rbuf[0:32, RF + i * 512:RF + (i + 1) * 512].bitcast(F32).rearrange("p (a f) -> p a f", a=2)
              for i in range(2)]
    convst = [Rbuf[0:18, RF + 1024 + i * 2048: RF + 1024 + (i + 1) * 2048].bitcast(F32) for i in range(2)]
    cs_s = Rbuf[:, RF + 5120: RF + 5120 + 64].bitcast(F32).rearrange("p (a b) -> p a b", a=2)
    assert RF + 5184 <= R_END1

    def fs_load(l, f):
        for ab in range(2):
            fc = f + ab * FA
            load("sp", scslab[f % 2][:, ab, :],
                 sconv_d[l][:, :, fc * 128:(fc + 1) * 128].rearrange("b j f -> (b j) f"), "scslab%d_%d" % (f % 2, ab))

    def ffn_samples(l, f, hs):
        hf = f // FH
        for ab in range(2):
            tr(psb[6][:, ab * 64:ab * 64 + 32], scslab[f % 2][:, ab, :], ident32[0:32, 0:32])
        for ab in range(2):
            tr(psb[7][0:18, ab * 128:(ab + 1) * 128], hs[:, ab, TP1:TP1 + 18], ident32[:, :])
        for ab in range(2):
            fc = f + ab * FA
            pv = psb[6][:, ab * 64:ab * 64 + 32].rearrange("p (b j) -> p b j", b=NS)
            w0 = cwT[:, l, 0, fc:fc + 1]
            w1 = cwT[:, l, 1, fc:fc + 1]
            w2 = cwT[:, l, 2, fc:fc + 1]
            bb = cwT[:, l, 3, fc:fc + 1]
            cs = cs_s[:, ab, :]
            ts(cs, hs[:, ab, 2 + TP1:2 + TP1 + NS], w2, bb, ALU.mult, ALU.add)
            stt(cs, pv[:, :, 1], w1, cs, ALU.mult, ALU.add)
            stt(cs, pv[:, :, 0], w0, cs, ALU.mult, ALU.add)
        for ab in range(2):
            slot = f % 8
            copy(convst[ab][:, slot * 128:(slot + 1) * 128], psb[7][0:18, ab * 128:(ab + 1) * 128], eng="act")
            if slot == 7 or f == FA - 1:
                f0 = (f // 8) * 8
                c0 = (f0 + ab * FA) * 128
                wdt = (slot + 1) * 128
                store(cvp_d[l, :, c0:c0 + wdt], convst[ab][0:2, 0:wdt])
                store(cvs_d[l, :, 1, c0:c0 + wdt], convst[ab][2:18, 0:wdt])
        act(cs_s[:, 0, :], cs_s[:, 0, :], GELU)
        tt(actT[:, f - hf * FH, TP1:TP1 + NS], cs_s[:, 0, :], cs_s[:, 1, :], ALU.mult)

    def conv_out(l):
        pass

    def final_norm(t):
        for (c, n, col0) in chunks_of(t):
            ss = scol()
            act(junk[0:n, :], xres[0:n, c, :], AF.Square, accum=ss[0:n])
            rt = scol()
            act(rt[0:n], ss[0:n], AF.Sqrt, scale=1.0 / D, bias=epsT[0:n])
            rs = scol()
            recip(rs[0:n], rt[0:n])
            stt(xres[0:n, c, :], xres[0:n, c, :], rs[0:n], gfin[0:n, :], ALU.mult, ALU.mult)
            if c != SCH:
                store(y_d[t * TP + c * 128: t * TP + (c + 1) * 128, :], xres[0:n, c, :])
            else:
                store(y_d[NCH * 128: NCH * 128 + NS, :], xres[0:n, c, :])

    gfin = Rbuf[:, 0:2 * D].bitcast(F32)

    load("sp", xres[:, 2:TC[0], :], xp_d[2 * 128:TC[0] * 128, :].rearrange("(c p) d -> p c d", p=128), "xload_b")
    mem_norm()
    for l in range(DEPTH):
        mem_phase(l)
    for t in range(2):
        if t == 0:
            load("sp", xres[:, 0:2, :], xp_d[0:2 * 128, :].rearrange("(c p) d -> p c d", p=128), "xload")
        else:
            load("sp", xres[:, 0:TC[t], :], xp_d[t * TP:t * TP + TC[t] * 128, :].rearrange("(c p) d -> p c d", p=128),
                 "xload")
        if t == 1 and with_samples:
            load("sp", xres[0:NS, SCH, :], xs_d, "xsload")
        for l in range(DEPTH):
            PHASES.append(("mix t%d l%d" % (t, l), len(pg.eng_ops["pe"])))
            mixer(l, t)
            PHASES.append(("ffn t%d l%d" % (t, l), len(pg.eng_ops["pe"])))
            ffn(l, t)
        load("sp", gfin, gains_d[G_FIN * KC:(G_FIN + 1) * KC, :].rearrange("(o k) p -> o (k p)", o=1)
             .to_broadcast([128, D]), "gfin")
        final_norm(t)

    pg.emit(stack)
    stack.close()
    return nc


_CONSTS = {}
PHASES = []


def _consts():
    if not _CONSTS:
        ident = np.eye(128, dtype=np.float32)
        s = np.arange(128)
        triu = (s[:, None] <= s[None, :]).astype(np.float32)
        j = np.arange(128)[:, None]
        i = np.arange(128)[None, :]
        d_prev = (i + 128 - j).astype(np.float32)
        d_prev = np.where(j >= i, d_prev, 1e5)
        d_cur = (i - j).astype(np.float32)
        d_cur = np.where(j <= i, d_cur, 1e5)
        distm = np.stack([d_prev, d_cur], axis=1).astype(np.float32)
        _CONSTS.update(ident=ident, triu=triu, distm=np.ascontiguousarray(distm))
    return _CONSTS


_NC = {}


def kernel(x_prompt, x_sample, cache_swa_k, cache_swa_v, cache_mem_k, cache_mem_v, state_conv,
           mem_prompt, norm_mix_g, w_in, gmlp_norm_g, gmlp_ws, gmlp_bs, attn_sinks, mem_norm_g,
           w_mem_kv, w_br_g, w_br_a, w_br_m, w_out, norm_ffn_g, w_up, conv_w, conv_b, w_down,
           final_norm_g):
    f = lambda a: np.ascontiguousarray(np.asarray(a, dtype=np.float32))
    x_prompt, x_sample = f(x_prompt), f(x_sample)
    cst = _consts()
    gains = np.concatenate([f(norm_mix_g).reshape(-1), f(norm_ffn_g).reshape(-1), f(mem_norm_g).reshape(-1),
                            f(final_norm_g).reshape(-1)]).reshape(7 * KC, 128)
    cwb = np.concatenate([f(conv_w), f(conv_b)[:, None, :]], axis=1).reshape(DEPTH, 4, FC, 128)
    shared = dict(
        gains=np.ascontiguousarray(gains), w_in=f(w_in), gmlp_norm_g=f(gmlp_norm_g), gmlp_ws=f(gmlp_ws),
        gmlp_bs=f(gmlp_bs).reshape(DEPTH, NG * 128), attn_sinks=f(attn_sinks).reshape(1, DEPTH * HQ),
        w_mem_kv=f(w_mem_kv), w_br_g=f(w_br_g), w_br_a=f(w_br_a), w_br_m=f(w_br_m), w_out=f(w_out),
        w_up=f(w_up), cwb=np.ascontiguousarray(cwb), w_down=f(w_down), ident=cst["ident"], triu=cst["triu"],
        distm=cst["distm"])
    in_maps = []
    for c in range(NCORES):
        b, half = c // 2, c % 2
        start = 0 if half == 0 else (16 - NCH) * 128
        if half == 0:
            xlight = np.zeros((128, D), np.float32)
            dist0 = np.full((128, 128), 1e5, np.float32)
        else:
            xlight = np.ascontiguousarray(x_prompt[b, start - 128:start])
            dist0 = np.ascontiguousarray(cst["distm"][:, 0, :])
        sl = slice(c * NS, (c + 1) * NS)
        m = dict(shared)
        m.update(
            xp=np.ascontiguousarray(x_prompt[b, start:start + NCH * 128]),
            xlight=xlight, dist0=dist0,
            xs=np.ascontiguousarray(x_sample[sl, 0]),
            memp=np.ascontiguousarray(f(mem_prompt)[b]),
            cswk=np.ascontiguousarray(f(cache_swa_k)[:, sl].reshape(DEPTH, NS, 128, 256)),
            cswv=np.ascontiguousarray(f(cache_swa_v)[:, sl].reshape(DEPTH, NS, 128, 256)),
            cmk=np.ascontiguousarray(f(cache_mem_k)[:, sl].reshape(DEPTH, NS, MEM, 512)),
            cmv=np.ascontiguousarray(f(cache_mem_v)[:, sl].reshape(DEPTH, NS, MEM, 512)),
            sconv=np.ascontiguousarray(f(state_conv)[:, sl]),
        )
        in_maps.append(m)
    if "nc" not in _NC:
        _NC["nc"] = build()
    res = run_bass_kernel_spmd(_NC["nc"], in_maps, core_ids=list(range(NCORES)))
    R = res.results
    B = 4
    y_prompt = np.zeros((B, 2048, D), np.float32)
    y_sample = np.zeros((128, 1, D), np.float32)
    kp = np.zeros((DEPTH, B, 128, HKV, HD), np.float32)
    vp = np.zeros_like(kp)
    ks = np.zeros((DEPTH, 128, 128, HKV, HD), np.float32)
    vs = np.zeros_like(ks)
    mko = np.zeros((DEPTH, B, MEM, MH, MHD), np.float32)
    mvo = np.zeros_like(mko)
    gvp = np.zeros((DEPTH, B, 128, GW), np.float32)
    gvs = np.zeros((DEPTH, 128, 1, GW), np.float32)
    cvp = np.zeros((DEPTH, B, 2, 2 * DFF), np.float32)
    cvs = np.zeros((DEPTH, 128, 2, 2 * DFF), np.float32)
    for c in range(NCORES):
        b, half = c // 2, c % 2
        r = R[c]
        sl = slice(c * NS, (c + 1) * NS)
        y = r["y"]
        if half == 0:
            y_prompt[b, 0:NCH * 128] = y[0:NCH * 128]
            mko[:, b] = r["mko"].reshape(DEPTH, MEM, MH, MHD)
            mvo[:, b] = r["mvo"].reshape(DEPTH, MEM, MH, MHD)
        else:
            keep = 2048 - NCH * 128
            y_prompt[b, NCH * 128:] = y[NCH * 128 - keep:NCH * 128]
            kp[:, b] = r["kp"].reshape(DEPTH, 128, HKV, HD)
            vp[:, b] = r["vp"].reshape(DEPTH, 128, HKV, HD)
            gvp[:, b] = r["gvp"]
            cvp[:, b] = r["cvp"]
        y_sample[sl, 0] = y[NCH * 128:]
        ks[:, sl] = r["ks"].reshape(DEPTH, NS, 128, HKV, HD)
        vs[:, sl] = r["vs"].reshape(DEPTH, NS, 128, HKV, HD)
        gvs[:, sl, 0] = r["gvs"]
        cvs[:, sl] = r["cvs"]
    return (y_prompt, y_sample, kp, vp, ks, vs, mko, mvo, gvp, gvs, cvp, cvs)
```

```python
import math
from contextlib import ExitStack
import numpy as np
import concourse.bass as bass
import concourse.mybir as mybir
from concourse.bass_utils import run_bass_kernel_spmd

F32 = mybir.dt.float32
BF16 = mybir.dt.bfloat16
AF = mybir.ActivationFunctionType
ALU = mybir.AluOpType
AX = mybir.AxisListType

DEPTH = 2
D = 2048
KC = 16
GW = 768
NG = 8
GD = 96
HQ = 12
HKV = 4
HD = 64
MH = 4
MHD = 128
MEM = 256
DFF = 5632
FC = 88
FA = 44
FH = 22
NIN = 9472
C_U, C_V, C_Q, C_K, C_VV, C_M, C_G = 0, 768, 1536, 2304, 2560, 2816, 3328
NCH = 9
TCH = 5
TP = TCH * 128
TC = (5, 4)
SCH = 5
TP1 = TC[1] * 128
NS = 16
NCORES = 8
EPS = 1e-6
GELU = AF.Gelu_apprx_tanh
ESZ = {F32: 4, BF16: 2}


def alibi_slopes(n):
    p = 2 ** int(math.floor(math.log2(n)))
    base = [2.0 ** (-8.0 * (i + 1) / p) for i in range(p)]
    extra = [2.0 ** (-8.0 * (2 * i + 1) / (2 * p)) for i in range(n - p)]
    return base + extra


class Op:
    __slots__ = ("id", "eng", "fn", "deps", "dma_key", "dma_cum", "signal", "sigcount")


class Prog:
    GRAN = 32
    ENGS = ("pe", "act", "dve", "pool", "sp")

    def __init__(self, nc):
        self.nc = nc
        self.ops = []
        self.cells = {}
        self.dma_tot = {}
        self.last_out = {}
        self.eng_ops = {e: [] for e in self.ENGS}

    @staticmethod
    def region(ap):
        name = ap.tensor.name
        esz = ESZ.get(ap.dtype, 4)
        dims = ap.ap
        pstride = dims[0][0]
        off = ap.offset % pstride if pstride > 0 else ap.offset
        ext = 1
        for st, n in dims[1:]:
            ext += (n - 1) * abs(st)
        return name, off * esz, (off + ext) * esz

    def _cells(self, ap):
        name, lo, hi = self.region(ap)
        if name.startswith("psb"):
            return [(name, 0)]
        g = self.GRAN
        return [(name, i) for i in range(lo // g, (hi - 1) // g + 1)]

    def op(self, eng, fn, reads=(), writes=(), dma_key=None):
        o = Op()
        o.id = len(self.ops)
        o.eng = eng
        o.fn = fn
        o.deps = set()
        o.dma_key = dma_key
        o.signal = False
        o.sigcount = 0
        o.dma_cum = 0
        if dma_key is not None:
            self.dma_tot[dma_key] = self.dma_tot.get(dma_key, 0) + 16
            o.dma_cum = self.dma_tot[dma_key]
            if dma_key.startswith("out") or dma_key.startswith("wk"):
                prev = self.last_out.get(dma_key)
                if prev is not None:
                    o.deps.add(prev)
                self.last_out[dma_key] = o.id
        reads = list(reads)
        writes = list(writes)
        writes = writes + [ap for ap in reads if ap.tensor.name.startswith("psb")]
        reads = [ap for ap in reads if not ap.tensor.name.startswith("psb")]
        for ap in reads:
            for c in self._cells(ap):
                st = self.cells.get(c)
                if st is None:
                    st = [None, []]
                    self.cells[c] = st
                if st[0] is not None:
                    o.deps.add(st[0])
                st[1].append(o.id)
        for ap in writes:
            for c in self._cells(ap):
                st = self.cells.get(c)
                if st is None:
                    st = [None, []]
                    self.cells[c] = st
                if st[0] is not None:
                    o.deps.add(st[0])
                for r in st[1]:
                    if r != o.id:
                        o.deps.add(r)
                st[0] = o.id
                st[1] = []
        o.deps.discard(o.id)
        self.ops.append(o)
        self.eng_ops[eng].append(o)
        return o

    def emit(self, stack):
        nc = self.nc
        ops = self.ops
        for o in ops:
            latest = {}
            for d in o.deps:
                a = ops[d]
                if a.dma_key is None and not (a.eng == "pe" and o.eng == "pe"):
                    if d > latest.get(a.eng, -1):
                        latest[a.eng] = d
            for d in latest.values():
                ops[d].signal = True
            o.deps = set(d for d in o.deps if ops[d].dma_key is not None or d in latest.values())
        cnt = {e: 0 for e in self.ENGS}
        for o in ops:
            if o.dma_key is None and o.signal:
                cnt[o.eng] += 1
                o.sigcount = cnt[o.eng]
        sem_eng = {e: stack.enter_context(nc.semaphore("s_" + e)) for e in self.ENGS}
        sem_dma = {k: stack.enter_context(nc.semaphore("d_" + k)) for k in self.dma_tot}
        self.final_waits = [(sem_dma[k], v) for k, v in self.dma_tot.items()]
        block = stack.enter_context(nc.Block())

        def run(engname, e):
            waited = {}
            for o in self.eng_ops[engname]:
                need = {}
                for d in o.deps:
                    a = ops[d]
                    if a.dma_key is not None:
                        key = ("d", a.dma_key)
                        val = a.dma_cum
                    else:
                        if a.eng == "pe" and engname == "pe":
                            continue
                        key = ("e", a.eng)
                        val = a.sigcount
                    if val > need.get(key, 0):
                        need[key] = val
                for key, val in need.items():
                    if waited.get(key, 0) >= val:
                        continue
                    waited[key] = val
                    sem = sem_dma[key[1]] if key[0] == "d" else sem_eng[key[1]]
                    e.wait_ge(sem, val)
                ins = o.fn(e)
                if o.dma_key is not None:
                    ins.then_inc(sem_dma[o.dma_key], 16)
                elif o.signal:
                    ins.then_inc(sem_eng[engname], 1)
            if engname == "sp":
                for sem, v in self.final_waits:
                    e.wait_ge(sem, v)

        @block.tensor
        def _(e):
            run("pe", e)

        @block.scalar
        def _(e):
            run("act", e)

        @block.vector
        def _(e):
            run("dve", e)

        @block.gpsimd
        def _(e):
            run("pool", e)

        @block.sync
        def _(e):
            run("sp", e)


def build(with_samples=True):
    nc = bass.Bass("TRN2", target_bir_lowering=False)
    nc.dge_precook = False
    pg = Prog(nc)
    slopes = alibi_slopes(HQ)

    def din(name, shape, dt=F32):
        return nc.dram_tensor(name, list(shape), dt, kind="ExternalInput").ap()

    def dout(name, shape):
        return nc.dram_tensor(name, list(shape), F32, kind="ExternalOutput").ap()

    xp_d = din("xp", [NCH * 128, D])
    xs_d = din("xs", [NS, D])
    memp_d = din("memp", [MEM, D])
    cswk_d = din("cswk", [DEPTH, NS, 128, 256])
    cswv_d = din("cswv", [DEPTH, NS, 128, 256])
    cmk_d = din("cmk", [DEPTH, NS, MEM, 512])
    cmv_d = din("cmv", [DEPTH, NS, MEM, 512])
    sconv_d = din("sconv", [DEPTH, NS, 2, 2 * DFF])
    gains_d = din("gains", [7 * KC, 128])
    w_in_d = din("w_in", [DEPTH, D, NIN])
    ggn_d = din("gmlp_norm_g", [DEPTH, GW])
    gws_d = din("gmlp_ws", [DEPTH, NG, 128, 128])
    gbs_d = din("gmlp_bs", [DEPTH, NG * 128])
    sinks_d = din("attn_sinks", [1, DEPTH * HQ])
    wmem_d = din("w_mem_kv", [DEPTH, D, 2 * 512])
    wbg_d = din("w_br_g", [DEPTH, GW, D])
    wba_d = din("w_br_a", [DEPTH, GW, D])
    wbm_d = din("w_br_m", [DEPTH, 512, D])
    wout_d = din("w_out", [DEPTH, D, D])
    wup_d = din("w_up", [DEPTH, D, 2 * DFF])
    cwb_d = din("cwb", [DEPTH, 4, FC, 128])
    wdn_d = din("w_down", [DEPTH, DFF, D])
    ident_d = din("ident", [128, 128])
    triu_d = din("triu", [128, 128])
    dist_d = din("distm", [128, 2, 128])
    dist0_d = din("dist0", [128, 128])
    xl_d = din("xlight", [128, D])
    y_d = dout("y", [NCH * 128 + NS, D])
    kp_d = dout("kp", [DEPTH, 128, 256])
    vp_d = dout("vp", [DEPTH, 128, 256])
    ks_d = dout("ks", [DEPTH, NS, 128, 256])
    vs_d = dout("vs", [DEPTH, NS, 128, 256])
    mko_d = dout("mko", [DEPTH, MEM, 512])
    mvo_d = dout("mvo", [DEPTH, MEM, 512])
    gvp_d = dout("gvp", [DEPTH, 128, GW])
    gvs_d = dout("gvs", [DEPTH, NS, GW])
    cvp_d = dout("cvp", [DEPTH, 2, 2 * DFF])
    cvs_d = dout("cvs", [DEPTH, NS, 2, 2 * DFF])
    qscr_d = nc.dram_tensor("qscr", [DEPTH, NS, GW + 512], F32).ap()

    stack = ExitStack()

    def sb(name, shape, dt):
        return stack.enter_context(nc.sbuf_tensor("sb_" + name, list(shape), dt))

    xres = sb("xres", [128, TCH + 1, D], F32)
    xnT = sb("xnT", [128, KC, TP + NS], BF16)
    NCOL = TP + NS
    R_U = 0
    R_OA = R_U + NG * NCOL
    R_OM = R_OA + 6 * NCOL
    R_MG = R_OM + 4 * NCOL
    R_VG = R_MG
    R_Q = R_VG + 6 * GW
    R_KA = R_Q + 6 * NCOL
    R_KB = R_KA + 2 * (128 + NCOL)
    R_V = R_KB + 2 * (128 + NCOL)
    R_QM = ((R_V + 7 * 4 * 66 + 15) // 16) * 16
    R_END1 = ((max(R_QM + 4 * NCOL, R_MG + KC * NCOL) + 15) // 16) * 16
    R_ACT = 0
    R_HST = R_ACT + FH * NCOL
    HSTN = ((2 * (2 + NCOL) * 2 + 15) // 16) * 16
    assert R_HST + 2 * HSTN <= R_END1
    R_SCR = R_END1
    R_END = R_SCR + 4 * NCOL * 2
    Rbuf = sb("Rbuf", [128, R_END], BF16)

    def rview(off, shape, parts=128):
        n = 1
        for s in shape:
            n *= s
        v = Rbuf[0:parts, off:off + n]
        if len(shape) == 2:
            return v.rearrange("p (a b) -> p a b", a=shape[0])
        if len(shape) == 3:
            return v.rearrange("p (a b c) -> p a b c", a=shape[0], b=shape[1])
        return v

    uT = rview(R_U, [NG, NCOL])
    oaT = rview(R_OA, [6, NCOL])
    omT = rview(R_OM, [4, NCOL])
    mgT = rview(R_MG, [KC, NCOL])
    vg = rview(R_VG, [6, GW])
    qT = rview(R_Q, [6, NCOL])
    kAT = rview(R_KA, [2, 128 + NCOL])
    kBT = rview(R_KB, [2, 128 + NCOL])
    v1 = rview(R_V, [7, 4, 66])
    qmT = rview(R_QM, [4, NCOL])
    actT = rview(R_ACT, [FH, NCOL])
    hst = [Rbuf[:, R_HST + i * HSTN: R_HST + i * HSTN + 2 * (2 + NCOL) * 2].bitcast(F32)
           .rearrange("p (a n) -> p a n", a=2) for i in range(2)]
    scr32 = [Rbuf[:, R_SCR + i * 2 * NCOL: R_SCR + (i + 1) * 2 * NCOL].bitcast(F32) for i in range(4)]
    xsbs = [Rbuf[:, R_SCR + 4 * NCOL: R_SCR + 4 * NCOL + D], Rbuf[:, R_SCR: R_SCR + D]]
    xsb_ctr = [0]
    junk = xsbs[0]

    NB = 6
    wslots = [sb("wslot%d" % i, [128, 4096], BF16) for i in range(NB)]
    small = sb("small", [128, 64], F32)
    gcols = sb("gcols", [128, 7 * KC], F32)
    cwT = sb("cwT", [128, DEPTH, 4, FC], F32)
    ggn = Rbuf[:, R_OA + 2048:R_OA + 2048 + 2 * GW].bitcast(F32)
    esink = sb("esink", [128, DEPTH * HQ], F32)
    ident = sb("identb", [128, 128], BF16)
    ident32 = sb("ident32", [128, 128], F32)
    triu = sb("triu", [128, 128], F32)
    distm = sb("distm", [128, 2, 128], F32)
    distm0 = sb("distm0", [128, 2, 128], F32)
    ones_b = sb("ones_b", [128, 128], BF16)
    epsT = sb("epsT", [128, 1], F32)
    wsT = sb("wsT", [128, NG, 128], BF16)
    bsrow = sb("bsrow", [1, NG * 128], BF16)
    mkT = sb("mkT", [128, DEPTH, MH, MEM], BF16)
    mv = sb("mv", [128, DEPTH, 2, 512], BF16)
    halo_kA = sb("halo_kA", [128, DEPTH, 2, 128], BF16)
    halo_kB = sb("halo_kB", [128, DEPTH, 2, 128], BF16)
    halo_v = sb("halo_v", [128, DEPTH, 4, 66], BF16)
    hprev = sb("hprev", [128, DEPTH, FC, 2], F32)
    Ebuf = [Rbuf[:, R_SCR + i * 256:R_SCR + (i + 1) * 256].rearrange("p (b t) -> p b t", b=2) for i in range(4)]
    Emem = [Rbuf[:, R_SCR + 1024 + i * 656:R_SCR + 1024 + (i + 1) * 656].rearrange("p (b t) -> p b t", b=2)
            for i in range(2)]
    oatok = Rbuf[:, R_SCR + 3648:R_SCR + 3648 + GW]
    rden = sb("rden", [128, 16], F32)
    rmem = [Rbuf[:, R_SCR + 2336 + i * 656:R_SCR + 2336 + (i + 1) * 656].bitcast(F32) for i in range(2)]
    stage32 = Rbuf[:, R_OA:R_OA + 2048].bitcast(F32)
    wsraw = stage32.rearrange("p (g s) -> p g s", g=NG)
    memtok = xres[:, 0:2, :]
    cwrow = stage32[0:FC, 0:512].rearrange("p (j f) -> p j f", j=4)
    gainrow = stage32[0:7 * KC, 512:640]

    qTs = sb("qTs", [64, HQ, NS], BF16)
    kTs = sb("kTs", [64, HKV, NS], BF16)
    vTs = sb("vTs", [64, HKV, NS], BF16)
    shiftO = sb("shiftO", [64, 128], BF16)
    biasS = sb("biasS", [128, HQ], F32)
    psb = [stack.enter_context(nc.psum_tensor("psb%d" % i, [128, 512], F32)) for i in range(8)]
    pctr = [0]

    def bank():
        b = psb[pctr[0] % 6]
        pctr[0] += 1
        return b

    smallctr = [0]

    def scol():
        i = smallctr[0] % 64
        smallctr[0] += 1
        return small[:, i:i + 1]

    dctr = [0]

    def dma(q, out, in_, key=None, track_r=(), track_w=(), slow=False):
        if key is None:
            key = "x%d" % dctr[0]
            dctr[0] += 1
        kw = {"allow_slow_non_contiguous": True} if slow else {}
        pg.op(q, lambda e: e.dma_start(out=out, in_=in_, **kw), reads=track_r, writes=track_w, dma_key=key)

    def load(q, out_sb, in_dram, key, slow=False):
        dma(q, out_sb, in_dram, key=key, track_w=[out_sb], slow=slow)

    octr = [0]

    def okey():
        octr[0] += 1
        return "out%d" % (octr[0] % 8)

    def store(out_dram, in_sb, slow=False):
        dma("sp", out_dram, in_sb, key=okey(), track_r=[in_sb], slow=slow)

    def mm(out, lhsT, rhs, start, stop):
        pg.op("pe", lambda e: e.matmul(out, lhsT=lhsT, rhs=rhs, start=start, stop=stop),
              reads=[lhsT, rhs], writes=[out])

    def tr(out, in_, idn):
        pg.op("pe", lambda e: e.transpose(out, in_, idn), reads=[in_, idn], writes=[out])

    def act(out, in_, func, scale=1.0, bias=None, accum=None):
        rd = [in_]
        wr = [out]
        kw = {}
        if bias is not None:
            kw["bias"] = bias
            if not isinstance(bias, float):
                rd.append(bias)
        if not isinstance(scale, float):
            rd.append(scale)
        if accum is not None:
            kw["accum_out"] = accum
            wr.append(accum)
        pg.op("act", lambda e: e.activation(out=out, in_=in_, func=func, scale=scale, **kw), reads=rd, writes=wr)

    def tt(out, in0, in1, op, eng="dve"):
        pg.op(eng, lambda e: e.tensor_tensor(out=out, in0=in0, in1=in1, op=op), reads=[in0, in1], writes=[out])

    def ts(out, in0, s1, s2, op0, op1=None, eng="dve"):
        rd = [in0] + [s for s in (s1, s2) if s is not None and not isinstance(s, float)]
        if op1 is None:
            pg.op(eng, lambda e: e.tensor_scalar(out=out, in0=in0, scalar1=s1, scalar2=None, op0=op0),
                  reads=rd, writes=[out])
        else:
            pg.op(eng, lambda e: e.tensor_scalar(out=out, in0=in0, scalar1=s1, scalar2=s2, op0=op0, op1=op1),
                  reads=rd, writes=[out])

    def stt(out, in0, scalar, in1, op0, op1, eng="dve"):
        rd = [in0, in1] + ([] if isinstance(scalar, float) else [scalar])
        pg.op(eng, lambda e: e.scalar_tensor_tensor(out=out, in0=in0, scalar=scalar, in1=in1, op0=op0, op1=op1),
              reads=rd, writes=[out])

    def copy(out, in_, eng="dve"):
        if eng == "act":
            act(out, in_, AF.Copy)
        else:
            pg.op(eng, lambda e: e.tensor_copy(out=out, in_=in_), reads=[in_], writes=[out])

    def recip(out, in_):
        pg.op("dve", lambda e: e.reciprocal(out=out, in_=in_), reads=[in_], writes=[out])

    def memset(ap, val, eng="dve"):
        pg.op(eng, lambda e: e.memset(ap, val), writes=[ap])

    wctr = [0]

    def wload(src, kc, ncols, parts=128):
        i = wctr[0] % NB
        wctr[0] += 1
        dst = wslots[i][0:parts, 0:kc * ncols].rearrange("p (k n) -> p k n", k=kc)
        load("pool", dst, src, key="w%d" % i)
        return dst

    load("sp", ident32[:], ident_d, "c0")
    load("sp", triu[:], triu_d, "c1")
    load("sp", distm[:], dist_d, "c2")
    load("sp", distm0[:, 0, :], dist0_d, "c2b")
    load("sp", distm0[:, 1, :], dist_d[:, 1, :], "c2b")
    load("sp", gainrow[:], gains_d, "c3")
    load("sp", esink[:], sinks_d.to_broadcast([128, DEPTH * HQ]), "c5")
    copy(ident[:], ident32[:])
    memset(ones_b[:], 1.0)
    memset(epsT[:], EPS)
    memset(hprev[:], 0.0)
    act(esink[:], esink[:], AF.Exp)
    memset(shiftO[:], 0.0)
    copy(shiftO[:, 64:128], ident32[0:64, 0:64])
    for h in range(HQ):
        ts(biasS[:, h:h + 1], distm[:, 0, 0:1], -float(slopes[h]), None, ALU.mult)
    pb = bank()
    tr(pb[:, 0:7 * KC], gainrow[:], ident32[0:7 * KC, 0:7 * KC])
    copy(gcols[:], pb[:, 0:7 * KC])
    for l in range(DEPTH):
        load("sp", cwrow[:], cwb_d[l].rearrange("j k p -> k j p"), "c7")
        pb = bank()
        for j in range(4):
            tr(pb[:, j * FC:(j + 1) * FC], cwrow[:, j, :], ident32[0:FC, 0:FC])
        copy(cwT[:, l, :, :], pb[:, 0:4 * FC].rearrange("p (j k) -> p j k", j=4))

    G_MIX, G_FFN, G_MEM, G_FIN = 0, 2, 4, 6

    def gcol(gi, k0, n):
        return gcols[:, gi * KC + k0: gi * KC + k0 + n]

    def chunks_of(t):
        ch = [(c, 128, c * 128) for c in range(TC[t])]
        if t == 1 and with_samples:
            ch.append((SCH, NS, TP1))
        return ch

    def ncols_of(t):
        return TC[t] * 128 + (NS if (t == 1 and with_samples) else 0)

    def segs_of(t):
        n = ncols_of(t)
        h = n // 2
        return [(0, h), (h, n - h)]

    def norm_T(src_rows, n, gi, dstT, col0, extra=()):
        targets = [(gi, dstT, col0)] + list(extra)
        xsb = xsbs[xsb_ctr[0] % 2]
        xsb_ctr[0] += 1
        ss = scol()
        act(xsb[0:n, :], src_rows, AF.Square, accum=ss[0:n])
        rt = scol()
        act(rt[0:n], ss[0:n], AF.Sqrt, scale=1.0 / D, bias=epsT[0:n])
        rs = scol()
        recip(rs[0:n], rt[0:n])
        ts(xsb[0:n, :], src_rows, rs[0:n], None, ALU.mult)
        for half in range(2):
            pb = bank()
            pbv = pb[:].bitcast(BF16)[:, 0:8 * 128].rearrange("p (k t) -> p k t", k=8)
            for k in range(8):
                kk = half * 8 + k
                tr(pbv[:, k, 0:n], xsb[0:n, kk * 128:(kk + 1) * 128], ident[0:n, 0:n])
            for (g_i, d_T, c_0) in targets:
                tt(d_T[:, half * 8:half * 8 + 8, c_0:c_0 + n], pbv[:, :, 0:n],
                   gcol(g_i, half * 8, 8).unsqueeze(2).to_broadcast([128, 8, n]), ALU.mult)
        return rs

    def lin_ws(t, wblk, c0, m, rhsT, nk, evac, kparts=128):
        for (s0, sn) in segs_of(t):
            pb = bank()
            for k in range(nk):
                mm(pb[0:m, 0:sn], wblk[0:kparts, k, c0:c0 + m], rhsT[0:kparts, k, s0:s0 + sn], k == 0, k == nk - 1)
            evac(pb[0:m, 0:sn], s0, sn)

    w_in_v = [w_in_d[l].rearrange("(k p) n -> p k n", p=128) for l in range(DEPTH)]

    def mem_norm():
        load("sp", memtok[:], memp_d.rearrange("(c p) d -> p c d", p=128), "memtok")
        for c in range(2):
            norm_T(memtok[:, c, :], 128, G_MEM, xnT, c * 128, extra=[(G_MEM + 1, xnT, MEM + c * 128)])

    def mem_phase(l):
        mo = l * MEM
        wv = wmem_d[l].rearrange("(k p) n -> p k n", p=128)
        for blk in range(4):
            wb = wload(wv[:, :, blk * 256:(blk + 1) * 256], KC, 256)
            if blk < 2:
                for hh in range(2):
                    h = blk * 2 + hh
                    pb = bank()
                    for k in range(KC):
                        mm(pb[:, 0:MEM], wb[:, k, hh * 128:(hh + 1) * 128], xnT[:, k, mo:mo + MEM], k == 0, k == KC - 1)
                    copy(mkT[:, l, h, :], pb[:, 0:MEM], eng="act")
            for c in range(2):
                pb = bank()
                for k in range(KC):
                    mm(pb[:, 0:256], xnT[:, k, mo + c * 128:mo + (c + 1) * 128], wb[:, k, :], k == 0, k == KC - 1)
                si = (blk * 2 + c) % 4
                st = stage32[:, si * 256:(si + 1) * 256]
                copy(st, pb[:, 0:256], eng="act")
                if blk < 2:
                    store(mko_d[l, c * 128:(c + 1) * 128, blk * 256:(blk + 1) * 256], st)
                else:
                    b2 = blk - 2
                    copy(mv[:, l, c, b2 * 256:(b2 + 1) * 256], pb[:, 0:256])
                    store(mvo_d[l, c * 128:(c + 1) * 128, b2 * 256:(b2 + 1) * 256], st)

    def mixer(l, t):
        chs = chunks_of(t)
        ncol = ncols_of(t)
        last_tile = (t == 1)
        load("sp", wsraw, gws_d[l].rearrange("g t s -> t g s"), "wsraw")
        load("sp", ggn, ggn_d[l:l + 1, :].to_broadcast([128, GW]), "ggn")
        load("pool", bsrow[:], gbs_d[l:l + 1, :], "bsrow")
        for half in range(2):
            pb = bank()
            pbv = pb[:, :].rearrange("p (g t) -> p g t", g=4)
            for g in range(4):
                tr(pbv[:, g, :], wsraw[:, half * 4 + g, :], ident32[:])
            tt(wsT[:, half * 4:half * 4 + 4, :], pbv, triu[:].unsqueeze(1).to_broadcast([128, 4, 128]), ALU.mult)
        wv = w_in_v[l]
        if l == 0 and t == 0:
            load("sp", xres[:, SCH, :], xl_d, "xlight")
            norm_T(xres[:, SCH, :], 128, G_MIX + l, xnT, 0)
            wbl = wload(wv[:, :, C_K: C_K + 256], KC, 256)
            for j in range(2):
                pb = bank()
                for k in range(KC):
                    mm(pb[:, 0:128], wbl[:, k, j * 128:(j + 1) * 128], xnT[:, k, 0:128], k == 0, k == KC - 1)
                copy(kAT[:, j, 0:128], pb[:, 0:128], eng="act")
            i = wctr[0] % NB
            wctr[0] += 1
            wbx = wslots[i][:, 0:KC * 256].rearrange("p (k n) -> p k n", k=KC)
            for hp in range(2):
                for hh in range(2):
                    src = wv[:, :, C_K + (hp * 2 + (1 - hh)) * 64: C_K + (hp * 2 + (1 - hh)) * 64 + 64]
                    load("pool", wbx[:, :, (hp * 2 + hh) * 64:(hp * 2 + hh) * 64 + 64], src, key="wkL%d" % (hp * 2 + hh))
            for j in range(2):
                pb = bank()
                for k in range(KC):
                    mm(pb[:, 0:128], wbx[:, k, j * 128:(j + 1) * 128], xnT[:, k, 0:128], k == 0, k == KC - 1)
                copy(kBT[:, j, 0:128], pb[:, 0:128], eng="act")
            wbl = wload(wv[:, :, C_VV: C_VV + 256], KC, 256)
            pb = bank()
            for k in range(KC):
                mm(pb[:, 0:256], xnT[:, k, 0:128], wbl[:, k, :], k == 0, k == KC - 1)
            copy(v1[:, 0, :, 0:64], pb[:, 0:256].rearrange("p (h d) -> p h d", h=HKV), eng="act")
            memset(v1[:, 0, :, 64:65], 1.0)
        for (c, n, col0) in chs:
            norm_T(xres[0:n, c, :], n, G_MIX + l, xnT, col0)
        for b in range(3):
            wb = wload(wv[:, :, C_V + b * 256: C_V + (b + 1) * 256], KC, 256)
            for (c, n, col0) in chs:
                pb = bank()
                for k in range(KC):
                    mm(pb[0:n, 0:256], xnT[:, k, col0:col0 + n], wb[:, k, :], k == 0, k == KC - 1)
                act(vg[0:n, c, b * 256:(b + 1) * 256], pb[0:n, 0:256], GELU)
        for (c, n, col0) in chs:
            ss = scol()
            act(junk[0:n, 0:GW], vg[0:n, c, :], AF.Square, accum=ss[0:n])
            rt = scol()
            act(rt[0:n], ss[0:n], AF.Sqrt, scale=1.0 / GW, bias=epsT[0:n])
            rs = scol()
            recip(rs[0:n], rt[0:n])
            is_out = last_tile and (c == TC[1] - 1 or c == SCH)
            if is_out:
                st = stage32[0:n, 0:GW]
                stt(st, vg[0:n, c, :], rs[0:n], ggn[0:n, :], ALU.mult, ALU.mult)
                if c == SCH:
                    store(gvs_d[l], st)
                else:
                    store(gvp_d[l], st)
            stt(vg[0:n, c, :], vg[0:n, c, :], rs[0:n], ggn[0:n, :], ALU.mult, ALU.mult)
        for b in range(4):
            wb = wload(wv[:, :, C_U + b * 192: C_U + (b + 1) * 192], KC, 192)
            for gg in range(2):
                g = b * 2 + gg
                lin_ws(t, wb, gg * GD, GD, xnT, KC,
                       lambda ps, s0, sn, g=g: act(uT[0:GD, g, s0:s0 + sn], ps, GELU))
        for b in range(3):
            wb = wload(wv[:, :, C_Q + b * 256: C_Q + (b + 1) * 256], KC, 256)
            for j in range(2):
                qc = b * 2 + j
                lin_ws(t, wb, j * 128, 128, xnT, KC,
                       lambda ps, s0, sn, qc=qc: act(qT[:, qc, s0:s0 + sn], ps, AF.Copy, scale=0.125))
            if last_tile and with_samples:
                for hh in range(4):
                    pb = bank()
                    for k in range(KC):
                        mm(pb[0:64, 0:NS], wb[:, k, hh * 64:(hh + 1) * 64], xnT[:, k, TP1:TP1 + NS], k == 0, k == KC - 1)
                    act(qTs[:, b * 4 + hh, :], pb[0:64, 0:NS], AF.Copy, scale=0.125)
        wb = wload(wv[:, :, C_K: C_K + 256], KC, 256)
        for j in range(2):
            lin_ws(t, wb, j * 128, 128, xnT, KC,
                   lambda ps, s0, sn, j=j: copy(kAT[:, j, 128 + s0:128 + s0 + sn], ps, eng="act"))
        if last_tile and with_samples:
            for hh in range(4):
                pb = bank()
                for k in range(KC):
                    mm(pb[0:64, 0:NS], wb[:, k, hh * 64:(hh + 1) * 64], xnT[:, k, TP1:TP1 + NS], k == 0, k == KC - 1)
                copy(kTs[:, hh, :], pb[0:64, 0:NS], eng="act")
        if last_tile:
            for (c, n, col0) in chs[TC[1] - 1:]:
                pb = bank()
                for k in range(KC):
                    mm(pb[0:n, 0:256], xnT[:, k, col0:col0 + n], wb[:, k, :], k == 0, k == KC - 1)
                st = stage32[0:n, 256:512]
                copy(st, pb[0:n, 0:256], eng="act")
                if c == SCH:
                    store(ks_d[l, :, 127, :], st)
                else:
                    store(kp_d[l], st)
        i = wctr[0] % NB
        wctr[0] += 1
        wbs = wslots[i][:, 0:KC * 256].rearrange("p (k n) -> p k n", k=KC)
        for hp in range(2):
            for hh in range(2):
                src = wv[:, :, C_K + (hp * 2 + (1 - hh)) * 64: C_K + (hp * 2 + (1 - hh)) * 64 + 64]
                load("pool", wbs[:, :, (hp * 2 + hh) * 64:(hp * 2 + hh) * 64 + 64], src, key="wk%d" % (hp * 2 + hh))
        for j in range(2):
            lin_ws(t, wbs, j * 128, 128, xnT, KC,
                   lambda ps, s0, sn, j=j: copy(kBT[:, j, 128 + s0:128 + s0 + sn], ps, eng="act"))
        wb = wload(wv[:, :, C_VV: C_VV + 256], KC, 256)
        if last_tile and with_samples:
            for hh in range(4):
                pb = bank()
                for k in range(KC):
                    mm(pb[0:64, 0:NS], wb[:, k, hh * 64:(hh + 1) * 64], xnT[:, k, TP1:TP1 + NS], k == 0, k == KC - 1)
                copy(vTs[:, hh, :], pb[0:64, 0:NS], eng="act")
        for (c, n, col0) in chs:
            pb = bank()
            for k in range(KC):
                mm(pb[0:n, 0:256], xnT[:, k, col0:col0 + n], wb[:, k, :], k == 0, k == KC - 1)
            if c != SCH:
                copy(v1[0:n, 1 + c, :, 0:64], pb[0:n, 0:256].rearrange("p (h d) -> p h d", h=HKV), eng="act")
                memset(v1[0:n, 1 + c, :, 64:65], 1.0)
            if last_tile and (c == TC[1] - 1 or c == SCH):
                st = stage32[0:n, 512:768]
                copy(st, pb[0:n, 0:256], eng="act")
                if c == SCH:
                    store(vs_d[l, :, 127, :], st)
                else:
                    store(vp_d[l], st)
        for b in range(2):
            wb = wload(wv[:, :, C_M + b * 256: C_M + (b + 1) * 256], KC, 256)
            for j in range(2):
                h = b * 2 + j
                lin_ws(t, wb, j * 128, 128, xnT, KC,
                       lambda ps, s0, sn, h=h: act(qmT[:, h, s0:s0 + sn], ps, AF.Copy, scale=MHD ** -0.5))
        PHASES.append(("  attn t%d l%d" % (t, l), len(pg.eng_ops["pe"])))
        if t == 1:
            copy(kAT[:, :, 0:128], halo_kA[:, l, :, :])
            copy(kBT[:, :, 0:128], halo_kB[:, l, :, :])
            copy(v1[:, 0, :, 0:65], halo_v[:, l, :, 0:65])
        for c in range(TC[t]):
            col0 = c * 128
            for half in range(2):
                pb = bank()
                for gg in range(4):
                    g = half * 4 + gg
                    o = pb[0:GD, gg * 128:(gg + 1) * 128]
                    mm(o, vg[:, c, g * GD:(g + 1) * GD], wsT[:, g, :], True, False)
                    mm(o, ones_b[0:1, 0:GD], bsrow[0:1, g * 128:(g + 1) * 128], False, True)
                tt(uT[0:GD, half * 4:half * 4 + 4, col0:col0 + 128],
                   pb[0:GD, :].rearrange("p (g t) -> p g t", g=4),
                   uT[0:GD, half * 4:half * 4 + 4, col0:col0 + 128], ALU.mult)
            first = (t == 0 and c == 0)
            has_prev = (not first) or l == 0
            dm = distm0 if (first and l == 0) else distm
            pos = [psb[6], psb[7]]
            blocks = ([0] if has_prev else []) + [1]
            b0 = blocks[0]
            nb_ = len(blocks)

            LAG = 3

            def qk(h):
                kv = h // 3
                base = (h % 2) * 64
                kT = kAT if (kv % 2) == (h % 2) else kBT
                E = Ebuf[h % 4]
                pb = bank()
                for kb in blocks:
                    kcol = col0 + kb * 128
                    mm(pb[:, kb * 128:(kb + 1) * 128], kT[base:base + 64, kv // 2, kcol:kcol + 128],
                       qT[base:base + 64, h // 2, col0:col0 + 128], True, True)
                pbv = pb[:, b0 * 128:(b0 + nb_) * 128].rearrange("p (b t) -> p b t", b=nb_)
                stt(pbv, dm[:, b0:b0 + nb_, :], -float(slopes[h]), pbv, ALU.mult, ALU.add)
                act(E[:, b0:b0 + nb_, :], pbv, AF.Exp)

            def pvh(h):
                kv = h // 3
                E = Ebuf[h % 4]
                po = pos[h // 6][:, (h % 6) * 65:(h % 6) * 65 + 65]
                for ii, kb in enumerate(blocks):
                    mm(po, E[:, kb, :], v1[:, c + kb, kv, 0:65], ii == 0, ii == nb_ - 1)

            for h in range(HQ + LAG):
                if h < HQ:
                    qk(h)
                if h >= LAG:
                    pvh(h - LAG)
            for half in range(2):
                pv = pos[half][:, 0:6 * 65].rearrange("p (h e) -> p h e", h=6)
                rd = rden[:, half * 6:half * 6 + 6]
                tt(rd, pv[:, :, 64], esink[:, l * HQ + half * 6: l * HQ + half * 6 + 6], ALU.add)
                recip(rd, rd)
                tt(oatok[:, half * 384:(half + 1) * 384].rearrange("p (h d) -> p h d", h=6), pv[:, :, 0:64],
                   rd.unsqueeze(2).to_broadcast([128, 6, 64]), ALU.mult)
            pb = bank()
            pbv = pb[:].bitcast(BF16)[:, 0:6 * 128].rearrange("p (k t) -> p k t", k=6)
            for k in range(6):
                tr(pbv[:, k, :], oatok[:, k * 128:(k + 1) * 128], ident[:])
            copy(oaT[:, :, col0:col0 + 128], pbv, eng="act")
        if t == 0:
            copy(halo_kA[:, l, :, :], kAT[:, :, 128 + TP - 128:128 + TP])
            copy(halo_kB[:, l, :, :], kBT[:, :, 128 + TP - 128:128 + TP])
            copy(halo_v[:, l, :, 0:65], v1[:, TCH, :, 0:65])
        PHASES.append(("  mema t%d l%d" % (t, l), len(pg.eng_ops["pe"])))
        for (s0, sn) in [(0, TC[t] * 64), (TC[t] * 64, TC[t] * 64)]:
            def ms(h):
                E = Emem[h % 2]
                for mt in range(2):
                    pb = bank()
                    mm(pb[:, 0:sn], mkT[:, l, h, mt * 128:(mt + 1) * 128], qmT[:, h, s0:s0 + sn], True, True)
                    act(E[:, mt, 0:sn], pb[:, 0:sn], AF.Exp)

            def mp(h):
                E = Emem[h % 2]
                pd = bank()
                po = bank()
                for mt in range(2):
                    mm(pd[:, 0:sn], ones_b[:, :], E[:, mt, 0:sn], mt == 0, mt == 1)
                for mt in range(2):
                    mm(po[:, 0:sn], mv[:, l, mt, h * 128:(h + 1) * 128], E[:, mt, 0:sn], mt == 0, mt == 1)
                rm = rmem[h % 2]
                recip(rm[:, 0:sn], pd[:, 0:sn])
                tt(omT[:, h, s0:s0 + sn], po[:, 0:sn], rm[:, 0:sn], ALU.mult)

            for h in range(MH + 1):
                if h < MH:
                    ms(h)
                if h >= 1:
                    mp(h - 1)
        if last_tile and with_samples:
            PHASES.append(("  sattn t%d l%d" % (t, l), len(pg.eng_ops["pe"])))
            sample_attention(l)
            PHASES.append(("  merge t%d l%d" % (t, l), len(pg.eng_ops["pe"])))
        if not (last_tile and with_samples):
            PHASES.append(("  merge t%d l%d" % (t, l), len(pg.eng_ops["pe"])))
        wbg_v = wbg_d[l].rearrange("(g p) n -> p g n", p=GD)
        wba_v = wba_d[l].rearrange("(k p) n -> p k n", p=128)
        wbm_v = wbm_d[l].rearrange("(k p) n -> p k n", p=128)
        srcs = [(uT, NG, GD, wbg_v), (oaT, 6, 128, wba_v), (omT, 4, 128, wbm_v)]
        for jb in range(8):
            cs = slice(jb * 256, (jb + 1) * 256)
            for b in range(3):
                src, nk, kp, wsrc = srcs[b]
                wbr = wload(wsrc[:, :, cs], nk, 256, parts=kp)
                wg = wload(wv[:, :, C_G + b * D + jb * 256: C_G + b * D + (jb + 1) * 256], KC, 256)
                for jj in range(2):
                    j = jb * 2 + jj
                    acc = scr32[1 + jj]
                    for (s0, sn) in segs_of(t):
                        pgb = bank()
                        for k in range(KC):
                            mm(pgb[:, 0:sn], wg[:, k, jj * 128:(jj + 1) * 128], xnT[:, k, s0:s0 + sn],
                               k == 0, k == KC - 1)
                        sg = scr32[0 if (jj == 0) else 3]
                        act(sg[:, s0:s0 + sn], pgb[:, 0:sn], AF.Sigmoid)
                        ppb = bank()
                        for k in range(nk):
                            mm(ppb[:, 0:sn], wbr[0:kp, k, jj * 128:(jj + 1) * 128], src[0:kp, k, s0:s0 + sn],
                               k == 0, k == nk - 1)
                        if b == 0:
                            tt(acc[:, s0:s0 + sn], ppb[:, 0:sn], sg[:, s0:s0 + sn], ALU.mult)
                        else:
                            tt(sg[:, s0:s0 + sn], ppb[:, 0:sn], sg[:, s0:s0 + sn], ALU.mult)
                            if b == 1:
                                tt(acc[:, s0:s0 + sn], acc[:, s0:s0 + sn], sg[:, s0:s0 + sn], ALU.add)
                            else:
                                tt(mgT[:, j, s0:s0 + sn], acc[:, s0:s0 + sn], sg[:, s0:s0 + sn], ALU.add)
        PHASES.append(("  wout t%d l%d" % (t, l), len(pg.eng_ops["pe"])))
        wo_v = wout_d[l].rearrange("(k p) n -> p k n", p=128)
        for cb in range(8):
            wb = wload(wo_v[:, :, cb * 256:(cb + 1) * 256], KC, 256)
            for (c, n, col0) in chs:
                pb = bank()
                for k in range(KC):
                    mm(pb[0:n, 0:256], mgT[:, k, col0:col0 + n], wb[:, k, :], k == 0, k == KC - 1)
                tt(xres[0:n, c, cb * 256:(cb + 1) * 256], xres[0:n, c, cb * 256:(cb + 1) * 256], pb[0:n, 0:256],
                   ALU.add)

    def sample_attention(l):
        AL = R_VG
        cur = [AL + 12288]

        def sm(parts, shape, dt):
            n = 1
            for x_ in shape:
                n *= x_
            nel = n * (2 if dt == F32 else 1)
            v = Rbuf[0:parts, cur[0]:cur[0] + nel]
            cur[0] += ((nel + 15) // 16) * 16
            if dt == F32:
                v = v.bitcast(F32)
            if len(shape) == 2:
                v = v.rearrange("p (a b) -> p a b", a=shape[0])
            elif len(shape) == 3:
                v = v.rearrange("p (a b c) -> p a b c", a=shape[0], b=shape[1])
            return v

        ws00 = sm(16, [NG], F32)
        sI = sm(16, [NG, NS], BF16)
        bs0 = sm(1, [NG, NS], BF16)
        load("sp", ws00, gws_d[l][:, 0, 0:1].rearrange("g o -> o g").to_broadcast([NS, NG]), "ws00", slow=True)
        tt(sI, ident32[0:NS, 0:NS].unsqueeze(1).to_broadcast([NS, NG, NS]),
           ws00.unsqueeze(2).to_broadcast([NS, NG, NS]), ALU.mult)
        copy(bs0, bsrow[0:1, :].rearrange("o (g t) -> o g t", g=NG)[:, :, 0:1].to_broadcast([1, NG, NS]))
        pb = bank()
        for g in range(NG):
            o = pb[0:GD, g * NS:(g + 1) * NS]
            mm(o, vg[0:NS, SCH, g * GD:(g + 1) * GD], sI[:, g, :], True, False)
            mm(o, ones_b[0:1, 0:GD], bs0[0:1, g, :], False, True)
        tt(uT[0:GD, :, TP1:TP1 + NS], pb[0:GD, 0:NG * NS].rearrange("p (g b) -> p g b", g=NG),
           uT[0:GD, :, TP1:TP1 + NS], ALU.mult)
        cur_s1 = cur[0]
        cur[0] = AL + 7168
        Es = sm(128, [NS, HQ], BF16)
        sbS = sm(128, [4, HQ], F32)

        def s2slot(sl):
            o = AL + sl * 3584
            return (Rbuf[:, o:o + 1024].bitcast(F32).rearrange("p (b f) -> p b f", b=2),
                    Rbuf[:, o + 1024:o + 2048].bitcast(F32).rearrange("p (b f) -> p b f", b=2),
                    Rbuf[:, o + 2048:o + 2560].rearrange("p (b f) -> p b f", b=2),
                    Rbuf[0:64, o + 2560:o + 3584].rearrange("p (b k j) -> p b k j", b=2, k=4))

        psS = psb[7]
        psO = psb[6]
        NG2 = NS // 2

        def s2_load(i):
            K32, V32, Vb, KTs = s2slot(i % 2)
            load("sp", K32, cswk_d[l, i * 2:(i + 1) * 2].rearrange("b j f -> j b f"), "k32_%d" % (i % 2))
            load("sp", V32, cswv_d[l, i * 2:(i + 1) * 2].rearrange("b j f -> j b f"), "v32_%d" % (i % 2))

        def s2_tr(i):
            K32, V32, Vb, KTs = s2slot(i % 2)
            copy(Vb, V32)
            for bb in range(2):
                pk = bank()
                for kv in range(HKV):
                    tr(pk[0:64, kv * 128:(kv + 1) * 128], K32[:, bb, kv * 64:(kv + 1) * 64], ident32[:, :])
                copy(KTs[:, bb, :, :], pk[0:64, :].rearrange("p (k j) -> p k j", k=4), eng="act")

        def s2_sc(i):
            K32, V32, Vb, KTs = s2slot(i % 2)
            pS = bank()
            for bb in range(2):
                b = i * 2 + bb
                for kv in range(HKV):
                    mm(pS[:, bb * HQ + kv * 3: bb * HQ + kv * 3 + 3], KTs[:, bb, kv, :], qTs[:, kv * 3:(kv + 1) * 3, b],
                       True, True)
            sb_ = sbS[:, (i % 2) * 2:(i % 2) * 2 + 2, :]
            tt(sb_, pS[:, 0:2 * HQ].rearrange("p (b h) -> p b h", b=2),
               biasS[:, :].unsqueeze(1).to_broadcast([128, 2, HQ]), ALU.add)
            act(Es[:, i * 2:(i + 1) * 2, :], sb_, AF.Exp)

        def s2_pv(i):
            K32, V32, Vb, KTs = s2slot(i % 2)
            for bb in range(2):
                b = i * 2 + bb
                for kv in range(HKV):
                    mm(psO[0:64, b * HQ + kv * 3: b * HQ + kv * 3 + 3], Vb[:, bb, kv * 64:(kv + 1) * 64],
                       Es[:, b, kv * 3:(kv + 1) * 3], True, True)

        s2_load(0)
        for i in range(NG2 + 2):
            if i + 1 < NG2:
                s2_load(i + 1)
            if 0 <= i - 2 < NG2:
                s2_pv(i - 2)
            if 0 <= i - 1 < NG2:
                s2_sc(i - 1)
            if i < NG2:
                s2_tr(i)
        pd = bank()
        mm(pd[:, 0:NS * HQ], ones_b[:, :], Es.rearrange("p b h -> p (b h)"), True, True)
        prodb = sm(64, [HQ, NS], BF16)
        tt(prodb.rearrange("p (k g) b -> p k g b", k=HKV), qTs[:, :, :].rearrange("p (k g) b -> p k g b", k=HKV),
           kTs[:, :, :].unsqueeze(2).to_broadcast([64, HKV, 3, NS]), ALU.mult)
        pn = bank()
        mm(pn[:, 0:HQ * NS], ones_b[0:64, :], prodb.rearrange("p h b -> p (h b)"), True, True)
        enew = sm(64, [HQ, NS], F32)
        act(enew, pn[0:64, 0:HQ * NS].rearrange("p (h b) -> p h b", h=HQ), AF.Exp)
        dtot = sm(64, [NS, HQ], F32)
        tt(dtot, pd[0:64, 0:NS * HQ].rearrange("p (b h) -> p b h", b=NS), enew.rearrange("p h b -> p b h"), ALU.add)
        tt(dtot, dtot, esink[0:64, l * HQ:(l + 1) * HQ].unsqueeze(1).to_broadcast([64, NS, HQ]), ALU.add)
        recip(dtot, dtot)
        tmpo = sm(64, [NS, HQ], F32)
        tt(tmpo.rearrange("p b (k g) -> p b k g", k=HKV),
           enew.rearrange("p (k g) b -> p b k g", k=HKV),
           vTs[:, :, :].rearrange("p k b -> p b k").unsqueeze(3).to_broadcast([64, NS, HKV, 3]), ALU.mult)
        tt(tmpo, tmpo, psO[0:64, 0:NS * HQ].rearrange("p (b h) -> p b h", b=NS), ALU.add)
        oTs = sm(64, [NS, HQ], BF16)
        tt(oTs, tmpo, dtot, ALU.mult)
        pb = bank()
        for k in range(6):
            o = pb[:, k * NS:(k + 1) * NS]
            mm(o, ident[0:64, :], oTs[:, :, 2 * k], True, False)
            mm(o, shiftO[:, :], oTs[:, :, 2 * k + 1], False, True)
        copy(oaT[:, :, TP1:TP1 + NS], pb[:, 0:6 * NS].rearrange("p (k b) -> p k b", k=6), eng="act")
        def s3slot(sl):
            o = AL + sl * 6144
            return (Rbuf[:, o:o + 2048].bitcast(F32).rearrange("p (m f) -> p m f", m=2),
                    Rbuf[:, o + 2048:o + 4096].bitcast(F32).rearrange("p (m f) -> p m f", m=2),
                    Rbuf[:, o + 4096:o + 5120].rearrange("p (m f) -> p m f", m=2),
                    Rbuf[:, o + 5120:o + 6144].rearrange("p (h m j) -> p h m j", h=MH, m=2))

        assert cur[0] <= AL + 12288
        cur[0] = cur_s1
        Em = sm(128, [2, NS, MH], BF16)
        psSm = psb[7]
        psOm = psb[6]

        def s3_load(b):
            Km32, Vm32, Vmb, KTm = s3slot(b % 2)
            load("sp", Km32, cmk_d[l, b].rearrange("(m p) f -> p m f", p=128), "km32_%d" % (b % 2))
            load("sp", Vm32, cmv_d[l, b].rearrange("(m p) f -> p m f", p=128), "vm32_%d" % (b % 2))

        def s3_tr(b):
            Km32, Vm32, Vmb, KTm = s3slot(b % 2)
            copy(Vmb, Vm32)
            for hp in range(2):
                pk = bank()
                for hh in range(2):
                    h = hp * 2 + hh
                    for mt in range(2):
                        tr(pk[:, (hh * 2 + mt) * 128:(hh * 2 + mt + 1) * 128], Km32[:, mt, h * 128:(h + 1) * 128],
                           ident32[:, :])
                copy(KTm[:, hp * 2:hp * 2 + 2, :, :], pk[:, :].rearrange("p (h m j) -> p h m j", h=2, m=2), eng="act")

        def s3_sc(b):
            Km32, Vm32, Vmb, KTm = s3slot(b % 2)
            pS = bank()
            for h in range(MH):
                for mt in range(2):
                    cidx = mt * MH + h
                    mm(pS[:, cidx:cidx + 1], KTm[:, h, mt, :], qmT[:, h, TP1 + b:TP1 + b + 1], True, True)
            act(Em[:, :, b, :], pS[:, 0:2 * MH].rearrange("p (m h) -> p m h", m=2), AF.Exp)

        def s3_pv(b):
            Km32, Vm32, Vmb, KTm = s3slot(b % 2)
            for h in range(MH):
                for mt in range(2):
                    mm(psOm[:, 256 + b * MH + h: 256 + b * MH + h + 1], Vmb[:, mt, h * 128:(h + 1) * 128],
                       Em[:, mt, b, h:h + 1], mt == 0, mt == 1)

        s3_load(0)
        for i in range(NS + 2):
            if i + 1 < NS:
                s3_load(i + 1)
            if 0 <= i - 2 < NS:
                s3_pv(i - 2)
            if 0 <= i - 1 < NS:
                s3_sc(i - 1)
            if i < NS:
                s3_tr(i)
        pdm = bank()
        for mt in range(2):
            mm(pdm[:, 0:NS * MH], ones_b[:, :], Em[:, mt, :, :].rearrange("p b h -> p (b h)"), mt == 0, mt == 1)
        rdm = sm(128, [NS, MH], F32)
        assert cur[0] <= R_QM
        recip(rdm, pdm[:, 0:NS * MH].rearrange("p (b h) -> p b h", b=NS))
        tt(omT[:, :, TP1:TP1 + NS].rearrange("p h b -> p b h"),
           psOm[:, 256:256 + NS * MH].rearrange("p (b h) -> p b h", b=NS), rdm, ALU.mult)
        dma("sp", ks_d[l, :, 0:127, :], cswk_d[l, :, 1:128, :], key=okey())
        dma("sp", vs_d[l, :, 0:127, :], cswv_d[l, :, 1:128, :], key=okey())
        dma("sp", cvs_d[l, :, 0, :], sconv_d[l, :, 1, :], key=okey())

    def ffn(l, t):
        chs = chunks_of(t)
        tpt = TC[t] * 128
        ncol = ncols_of(t)
        last_tile = (t == 1)
        for (c, n, col0) in chs:
            norm_T(xres[0:n, c, :], n, G_FFN + l, xnT, col0)
        wu_v = wup_d[l].rearrange("(k p) n -> p k n", p=128)
        wd_v = wdn_d[l].rearrange("(k p) n -> p k n", p=128)
        for hf in range(2):
            pending = None
            for f in range(hf * FH, (hf + 1) * FH):
                if last_tile and with_samples:
                    fs_load(l, f)
                if f % 2 == 0:
                    wa = wload(wu_v[:, :, f * 128:(f + 2) * 128], KC, 256)
                    wbb = wload(wu_v[:, :, DFF + f * 128: DFF + (f + 2) * 128], KC, 256)
                hs = hst[f % 2]
                for ab, wb in enumerate((wa, wbb)):
                    fc = f + ab * FA
                    copy(hs[:, ab, 0:2], hprev[:, l, fc, :], eng="act")
                    lin_ws(t, wb, (f % 2) * 128, 128, xnT, KC,
                           lambda ps, s0, sn, ab=ab, hs=hs: copy(hs[:, ab, 2 + s0:2 + s0 + sn], ps, eng="act"))
                    copy(hprev[:, l, fc, :], hs[:, ab, tpt:tpt + 2], eng="act")
                    w0 = cwT[:, l, 0, fc:fc + 1]
                    w1 = cwT[:, l, 1, fc:fc + 1]
                    w2 = cwT[:, l, 2, fc:fc + 1]
                    bb = cwT[:, l, 3, fc:fc + 1]
                    cv = scr32[ab]
                    ts(cv[:, 0:tpt], hs[:, ab, 2:2 + tpt], w2, bb, ALU.mult, ALU.add)
                    stt(cv[:, 0:tpt], hs[:, ab, 1:1 + tpt], w1, cv[:, 0:tpt], ALU.mult, ALU.add)
                    stt(cv[:, 0:tpt], hs[:, ab, 0:tpt], w0, cv[:, 0:tpt], ALU.mult, ALU.add)
                act(scr32[0][:, 0:tpt], scr32[0][:, 0:tpt], GELU)
                tt(actT[:, f - hf * FH, 0:tpt], scr32[0][:, 0:tpt], scr32[1][:, 0:tpt], ALU.mult)
                if last_tile and with_samples:
                    if pending is not None:
                        ffn_samples(l, pending[0], pending[1])
                    pending = (f, hs)
            if pending is not None:
                ffn_samples(l, pending[0], pending[1])
            for cb in range(8):
                accs = {}
                for kg in range(2):
                    wb = wload(wd_v[:, hf * FH + kg * 11: hf * FH + (kg + 1) * 11, cb * 256:(cb + 1) * 256], 11, 256)
                    for (c, n, col0) in chs:
                        if kg == 0:
                            accs[c] = bank()
                        pb = accs[c]
                        for k in range(11):
                            kk = kg * 11 + k
                            mm(pb[0:n, 0:256], actT[:, kk, col0:col0 + n], wb[:, k, :], kk == 0, kk == FH - 1)
                for (c, n, col0) in chs:
                    tt(xres[0:n, c, cb * 256:(cb + 1) * 256], xres[0:n, c, cb * 256:(cb + 1) * 256],
                       accs[c][0:n, 0:256], ALU.add)

    RF = R_HST + 2 * HSTN
    scslab = [Rbuf[0:32, RF + i * 512:RF + (i + 1) * 512].bitcast(F32).rearrange("p (a f) -> p a f", a=2)
              for i in range(2)]
    convst = [Rbuf[0:18, RF + 1024 + i * 2048: RF + 1024 + (i + 1) * 2048].bitcast(F32) for i in range(2)]
    cs_s = Rbuf[:, RF + 5120: RF + 5120 + 64].bitcast(F32).rearrange("p (a b) -> p a b", a=2)
    assert RF + 5184 <= R_END1

    def fs_load(l, f):
        for ab in range(2):
            fc = f + ab * FA
            load("sp", scslab[f % 2][:, ab, :],
                 sconv_d[l][:, :, fc * 128:(fc + 1) * 128].rearrange("b j f -> (b j) f"), "scslab%d_%d" % (f % 2, ab))

    def ffn_samples(l, f, hs):
        hf = f // FH
        for ab in range(2):
            tr(psb[6][:, ab * 64:ab * 64 + 32], scslab[f % 2][:, ab, :], ident32[0:32, 0:32])
        for ab in range(2):
            tr(psb[7][0:18, ab * 128:(ab + 1) * 128], hs[:, ab, TP1:TP1 + 18], ident32[:, :])
        for ab in range(2):
            fc = f + ab * FA
            pv = psb[6][:, ab * 64:ab * 64 + 32].rearrange("p (b j) -> p b j", b=NS)
            w0 = cwT[:, l, 0, fc:fc + 1]
            w1 = cwT[:, l, 1, fc:fc + 1]
            w2 = cwT[:, l, 2, fc:fc + 1]
            bb = cwT[:, l, 3, fc:fc + 1]
            cs = cs_s[:, ab, :]
            ts(cs, hs[:, ab, 2 + TP1:2 + TP1 + NS], w2, bb, ALU.mult, ALU.add)
            stt(cs, pv[:, :, 1], w1, cs, ALU.mult, ALU.add)
            stt(cs, pv[:, :, 0], w0, cs, ALU.mult, ALU.add)
        for ab in range(2):
            slot = f % 8
            copy(convst[ab][:, slot * 128:(slot + 1) * 128], psb[7][0:18, ab * 128:(ab + 1) * 128], eng="act")
            if slot == 7 or f == FA - 1:
                f0 = (f // 8) * 8
                c0 = (f0 + ab * FA) * 128
                wdt = (slot + 1) * 128
                store(cvp_d[l, :, c0:c0 + wdt], convst[ab][0:2, 0:wdt])
                store(cvs_d[l, :, 1, c0:c0 + wdt], convst[ab][2:18, 0:wdt])
        act(cs_s[:, 0, :], cs_s[:, 0, :], GELU)
        tt(actT[:, f - hf * FH, TP1:TP1 + NS], cs_s[:, 0, :], cs_s[:, 1, :], ALU.mult)

    def conv_out(l):
        pass

    def final_norm(t):
        for (c, n, col0) in chunks_of(t):
            ss = scol()
            act(junk[0:n, :], xres[0:n, c, :], AF.Square, accum=ss[0:n])
            rt = scol()
            act(rt[0:n], ss[0:n], AF.Sqrt, scale=1.0 / D, bias=epsT[0:n])
            rs = scol()
            recip(rs[0:n], rt[0:n])
            stt(xres[0:n, c, :], xres[0:n, c, :], rs[0:n], gfin[0:n, :], ALU.mult, ALU.mult)
            if c != SCH:
                store(y_d[t * TP + c * 128: t * TP + (c + 1) * 128, :], xres[0:n, c, :])
            else:
                store(y_d[NCH * 128: NCH * 128 + NS, :], xres[0:n, c, :])

    gfin = Rbuf[:, 0:2 * D].bitcast(F32)

    load("sp", xres[:, 2:TC[0], :], xp_d[2 * 128:TC[0] * 128, :].rearrange("(c p) d -> p c d", p=128), "xload_b")
    mem_norm()
    for l in range(DEPTH):
        mem_phase(l)
    for t in range(2):
        if t == 0:
            load("sp", xres[:, 0:2, :], xp_d[0:2 * 128, :].rearrange("(c p) d -> p c d", p=128), "xload")
        else:
            load("sp", xres[:, 0:TC[t], :], xp_d[t * TP:t * TP + TC[t] * 128, :].rearrange("(c p) d -> p c d", p=128),
                 "xload")
        if t == 1 and with_samples:
            load("sp", xres[0:NS, SCH, :], xs_d, "xsload")
        for l in range(DEPTH):
            PHASES.append(("mix t%d l%d" % (t, l), len(pg.eng_ops["pe"])))
            mixer(l, t)
            PHASES.append(("ffn t%d l%d" % (t, l), len(pg.eng_ops["pe"])))
            ffn(l, t)
        load("sp", gfin, gains_d[G_FIN * KC:(G_FIN + 1) * KC, :].rearrange("(o k) p -> o (k p)", o=1)
             .to_broadcast([128, D]), "gfin")
        final_norm(t)

    pg.emit(stack)
    stack.close()
    return nc


_CONSTS = {}
PHASES = []


def _consts():
    if not _CONSTS:
        ident = np.eye(128, dtype=np.float32)
        s = np.arange(128)
        triu = (s[:, None] <= s[None, :]).astype(np.float32)
        j = np.arange(128)[:, None]
        i = np.arange(128)[None, :]
        d_prev = (i + 128 - j).astype(np.float32)
        d_prev = np.where(j >= i, d_prev, 1e5)
        d_cur = (i - j).astype(np.float32)
        d_cur = np.where(j <= i, d_cur, 1e5)
        distm = np.stack([d_prev, d_cur], axis=1).astype(np.float32)
        _CONSTS.update(ident=ident, triu=triu, distm=np.ascontiguousarray(distm))
    return _CONSTS


_NC = {}


def kernel(x_prompt, x_sample, cache_swa_k, cache_swa_v, cache_mem_k, cache_mem_v, state_conv,
           mem_prompt, norm_mix_g, w_in, gmlp_norm_g, gmlp_ws, gmlp_bs, attn_sinks, mem_norm_g,
           w_mem_kv, w_br_g, w_br_a, w_br_m, w_out, norm_ffn_g, w_up, conv_w, conv_b, w_down,
           final_norm_g):
    f = lambda a: np.ascontiguousarray(np.asarray(a, dtype=np.float32))
    x_prompt, x_sample = f(x_prompt), f(x_sample)
    cst = _consts()
    gains = np.concatenate([f(norm_mix_g).reshape(-1), f(norm_ffn_g).reshape(-1), f(mem_norm_g).reshape(-1),
                            f(final_norm_g).reshape(-1)]).reshape(7 * KC, 128)
    cwb = np.concatenate([f(conv_w), f(conv_b)[:, None, :]], axis=1).reshape(DEPTH, 4, FC, 128)
    shared = dict(
        gains=np.ascontiguousarray(gains), w_in=f(w_in), gmlp_norm_g=f(gmlp_norm_g), gmlp_ws=f(gmlp_ws),
        gmlp_bs=f(gmlp_bs).reshape(DEPTH, NG * 128), attn_sinks=f(attn_sinks).reshape(1, DEPTH * HQ),
        w_mem_kv=f(w_mem_kv), w_br_g=f(w_br_g), w_br_a=f(w_br_a), w_br_m=f(w_br_m), w_out=f(w_out),
        w_up=f(w_up), cwb=np.ascontiguousarray(cwb), w_down=f(w_down), ident=cst["ident"], triu=cst["triu"],
        distm=cst["distm"])
    in_maps = []
    for c in range(NCORES):
        b, half = c // 2, c % 2
        start = 0 if half == 0 else (16 - NCH) * 128
        if half == 0:
            xlight = np.zeros((128, D), np.float32)
            dist0 = np.full((128, 128), 1e5, np.float32)
        else:
            xlight = np.ascontiguousarray(x_prompt[b, start - 128:start])
            dist0 = np.ascontiguousarray(cst["distm"][:, 0, :])
        sl = slice(c * NS, (c + 1) * NS)
        m = dict(shared)
        m.update(
            xp=np.ascontiguousarray(x_prompt[b, start:start + NCH * 128]),
            xlight=xlight, dist0=dist0,
            xs=np.ascontiguousarray(x_sample[sl, 0]),
            memp=np.ascontiguousarray(f(mem_prompt)[b]),
            cswk=np.ascontiguousarray(f(cache_swa_k)[:, sl].reshape(DEPTH, NS, 128, 256)),
            cswv=np.ascontiguousarray(f(cache_swa_v)[:, sl].reshape(DEPTH, NS, 128, 256)),
            cmk=np.ascontiguousarray(f(cache_mem_k)[:, sl].reshape(DEPTH, NS, MEM, 512)),
            cmv=np.ascontiguousarray(f(cache_mem_v)[:, sl].reshape(DEPTH, NS, MEM, 512)),
            sconv=np.ascontiguousarray(f(state_conv)[:, sl]),
        )
        in_maps.append(m)
    if "nc" not in _NC:
        _NC["nc"] = build()
    res = run_bass_kernel_spmd(_NC["nc"], in_maps, core_ids=list(range(NCORES)))
    R = res.results
    B = 4
    y_prompt = np.zeros((B, 2048, D), np.float32)
    y_sample = np.zeros((128, 1, D), np.float32)
    kp = np.zeros((DEPTH, B, 128, HKV, HD), np.float32)
    vp = np.zeros_like(kp)
    ks = np.zeros((DEPTH, 128, 128, HKV, HD), np.float32)
    vs = np.zeros_like(ks)
    mko = np.zeros((DEPTH, B, MEM, MH, MHD), np.float32)
    mvo = np.zeros_like(mko)
    gvp = np.zeros((DEPTH, B, 128, GW), np.float32)
    gvs = np.zeros((DEPTH, 128, 1, GW), np.float32)
    cvp = np.zeros((DEPTH, B, 2, 2 * DFF), np.float32)
    cvs = np.zeros((DEPTH, 128, 2, 2 * DFF), np.float32)
    for c in range(NCORES):
        b, half = c // 2, c % 2
        r = R[c]
        sl = slice(c * NS, (c + 1) * NS)
        y = r["y"]
        if half == 0:
            y_prompt[b, 0:NCH * 128] = y[0:NCH * 128]
            mko[:, b] = r["mko"].reshape(DEPTH, MEM, MH, MHD)
            mvo[:, b] = r["mvo"].reshape(DEPTH, MEM, MH, MHD)
        else:
            keep = 2048 - NCH * 128
            y_prompt[b, NCH * 128:] = y[NCH * 128 - keep:NCH * 128]
            kp[:, b] = r["kp"].reshape(DEPTH, 128, HKV, HD)
            vp[:, b] = r["vp"].reshape(DEPTH, 128, HKV, HD)
            gvp[:, b] = r["gvp"]
            cvp[:, b] = r["cvp"]
        y_sample[sl, 0] = y[NCH * 128:]
        ks[:, sl] = r["ks"].reshape(DEPTH, NS, 128, HKV, HD)
        vs[:, sl] = r["vs"].reshape(DEPTH, NS, 128, HKV, HD)
        gvs[:, sl, 0] = r["gvs"]
        cvs[:, sl] = r["cvs"]
    return (y_prompt, y_sample, kp, vp, ks, vs, mko, mvo, gvp, gvs, cvp, cvs)
```
